# Optimizing a Trainium2 kernel written in Bass

```python
import math
import jax, jax.numpy as jnp
from jax import lax
import numpy as np

D_MODEL = 1024
BATCH = 4
SEQ = 8192
DEPTH = 4

N_A = DEPTH // 2
N_B = DEPTH - N_A
EXPAND = 2
A_WIDTH = EXPAND * D_MODEL
POOL_WINDOWS = (2, 4, 8, 16)
N_POOL_GROUPS = len(POOL_WINDOWS)
GROUP_WIDTH = A_WIDTH // N_POOL_GROUPS
HEAD_DIM = 64
N_HEADS = D_MODEL // HEAD_DIM
N_KV_HEADS = max(1, N_HEADS // 8)
GQA_GROUPS = N_HEADS // N_KV_HEADS
B_WIDTH = N_HEADS * HEAD_DIM
WINDOW = 128
BLOCK = 128
N_BUCKETS = 32
MAX_DISTANCE = 128
EPS = 1e-6
NEG_INF = -1e30

kernel_name = "yoco_pool_swa_sink_hybrid"


def rmsnorm(x, g):
    xf = x.astype(jnp.float32)
    y = xf * lax.rsqrt(jnp.mean(xf * xf, axis=-1, keepdims=True) + EPS)
    return y.astype(x.dtype) * g


def modulate(h, shift, scale):
    return h * (1 + scale[:, None, :]) + shift[:, None, :]


def t5_causal_buckets():
    i = np.arange(BLOCK)[:, None]
    j = np.arange(2 * BLOCK)[None, :]
    n = np.maximum(i + BLOCK - j, 0)
    max_exact = N_BUCKETS // 2
    large = max_exact + (np.log(np.maximum(n, 1) / max_exact) / math.log(MAX_DISTANCE / max_exact)
                         * (N_BUCKETS - max_exact)).astype(np.int32)
    large = np.minimum(large, N_BUCKETS - 1)
    return np.where(n < max_exact, n, large).astype(np.int32)


def band_mask(n_blocks):
    i = jnp.arange(BLOCK)[:, None]
    j = jnp.arange(2 * BLOCK)[None, :]
    rel = i + BLOCK - j
    band = (rel >= 0) & (rel < WINDOW)
    blk = jnp.arange(n_blocks)[:, None, None]
    return band[None] & ((blk > 0) | (j[None] >= BLOCK))


def causal_multiscale_pool(u):
    S = u.shape[1]
    uf = u.astype(jnp.float32)
    csp = jnp.pad(jnp.cumsum(uf, axis=1), ((0, 0), (1, 0), (0, 0)))
    t = jnp.arange(S)
    outs = []
    for g, w in enumerate(POOL_WINDOWS):
        c_g = csp[..., g * GROUP_WIDTH:(g + 1) * GROUP_WIDTH]
        upper = c_g[:, 1:]
        lower = jnp.pad(c_g, ((0, 0), (w, 0), (0, 0)))[:, 1:S + 1]
        count = jnp.minimum(t + 1, w).astype(jnp.float32)[None, :, None]
        outs.append((upper - lower) / count)
    return (jnp.concatenate(outs, axis=-1) - uf).astype(u.dtype)


def pool_mixer(h, w_in, w_group, scale, w_out):
    B, S, _ = h.shape
    u, z = jnp.split(h @ w_in, 2, axis=-1)
    p = causal_multiscale_pool(u).reshape(B, S, N_POOL_GROUPS, GROUP_WIDTH)
    y = jnp.einsum('bsgc,gcd->bsgd', p, w_group).reshape(B, S, A_WIDTH) * scale
    return (y * jax.nn.silu(z)) @ w_out


def shared_kv_blocks(x, c_act, kv_norm_g, kv_ada_w, kv_ada_b, w_kv):
    B, S, _ = x.shape
    nb = S // BLOCK
    shift, scale = jnp.split(c_act @ kv_ada_w + kv_ada_b, 2, axis=-1)
    hk = modulate(rmsnorm(x, kv_norm_g), shift, scale)
    kv = (hk @ w_kv).reshape(B, S, 2, N_KV_HEADS, HEAD_DIM)
    k, v = kv[:, :, 0], kv[:, :, 1]

    def to_band(t):
        prev = jnp.pad(t, ((0, 0), (BLOCK, 0), (0, 0), (0, 0)))[:, :S]
        prev = prev.reshape(B, nb, BLOCK, N_KV_HEADS, HEAD_DIM)
        cur = t.reshape(B, nb, BLOCK, N_KV_HEADS, HEAD_DIM)
        return jnp.concatenate([prev, cur], axis=2).transpose(1, 0, 2, 3, 4)

    return to_band(k), to_band(v)


def swa_sink_mixer(h, kb, vb, pos_bias, mask, w_in, sinks, w_out):
    B, S, _ = h.shape
    nb = S // BLOCK
    q, z = jnp.split(h @ w_in, 2, axis=-1)
    q = q * (HEAD_DIM ** -0.5)
    qb = q.reshape(B, nb, BLOCK, N_KV_HEADS, GQA_GROUPS, HEAD_DIM).transpose(1, 0, 2, 3, 4, 5)
    sink = sinks.astype(jnp.float32).reshape(N_KV_HEADS, GQA_GROUPS)[None, :, :, None, None]

    def block_fn(args):
        qn, kn, vn, mn = args
        s = jnp.einsum('bqhgd,bkhd->bhgqk', qn, kn).astype(jnp.float32) + pos_bias
        s = jnp.where(mn[None, None, None], s, NEG_INF)
        m = jnp.maximum(jnp.max(s, axis=-1, keepdims=True), sink)
        p = jnp.exp(s - m)
        p = p / (jnp.sum(p, axis=-1, keepdims=True) + jnp.exp(sink - m))
        return jnp.einsum('bhgqk,bkhd->bqhgd', p.astype(vn.dtype), vn)

    o = lax.map(block_fn, (qb, kb, vb, mask))
    o = o.transpose(1, 0, 2, 3, 4, 5).reshape(B, S, B_WIDTH)
    return (o * jax.nn.silu(z)) @ w_out


def setup_inputs(seed: int = 0) -> dict:
    key = jax.random.key(seed)
    ks = jax.random.split(key, 18)
    D, E, GW = D_MODEL, A_WIDTH, GROUP_WIDTH
    nrm = jax.random.normal
    f32 = jnp.float32
    return {
        "x": nrm(ks[0], (BATCH, SEQ, D), f32),
        "c": nrm(ks[1], (BATCH, D), f32),
        "norm_g": 1.0 + 0.05 * nrm(ks[2], (DEPTH, D), f32),
        "ada_w": 0.5 * D ** -0.5 * nrm(ks[3], (DEPTH, D, 3 * D), f32),
        "ada_b": 0.02 * nrm(ks[4], (DEPTH, 3 * D), f32),
        "a_w_in": D ** -0.5 * nrm(ks[5], (N_A, D, 2 * E), f32),
        "a_w_group": GW ** -0.5 * nrm(ks[6], (N_A, N_POOL_GROUPS, GW, GW), f32),
        "a_scale": 1.0 + 0.1 * nrm(ks[7], (N_A, E), f32),
        "a_w_out": E ** -0.5 * nrm(ks[8], (N_A, E, D), f32),
        "kv_norm_g": 1.0 + 0.05 * nrm(ks[9], (D,), f32),
        "kv_ada_w": 0.5 * D ** -0.5 * nrm(ks[10], (D, 2 * D), f32),
        "kv_ada_b": 0.02 * nrm(ks[11], (2 * D,), f32),
        "w_kv": D ** -0.5 * nrm(ks[12], (D, 2 * N_KV_HEADS * HEAD_DIM), f32),
        "b_w_in": D ** -0.5 * nrm(ks[13], (N_B, D, 2 * B_WIDTH), f32),
        "b_sinks": 0.5 * nrm(ks[14], (N_B, N_HEADS), f32),
        "b_w_out": B_WIDTH ** -0.5 * nrm(ks[15], (N_B, B_WIDTH, D), f32),
        "rel_bias": 0.5 * nrm(ks[16], (N_BUCKETS, N_HEADS), f32),
        "final_g": 1.0 + 0.05 * nrm(ks[17], (D,), f32),
    }


def reference(x, c, norm_g, ada_w, ada_b, a_w_in, a_w_group, a_scale, a_w_out,
              kv_norm_g, kv_ada_w, kv_ada_b, w_kv, b_w_in, b_sinks, b_w_out,
              rel_bias, final_g):
    S = x.shape[1]
    nb = S // BLOCK
    c_act = jax.nn.silu(c)
    buckets = t5_causal_buckets()
    pos_bias = rel_bias[buckets].astype(jnp.float32).transpose(2, 0, 1)
    pos_bias = pos_bias.reshape(N_KV_HEADS, GQA_GROUPS, BLOCK, 2 * BLOCK)[None]
    mask = band_mask(nb)

    kb = vb = None
    for l in range(DEPTH):
        shift, scale, gate = jnp.split(c_act @ ada_w[l] + ada_b[l], 3, axis=-1)
        if l < N_A:
            h = modulate(rmsnorm(x, norm_g[l]), shift, scale)
            y = pool_mixer(h, a_w_in[l], a_w_group[l], a_scale[l], a_w_out[l])
        else:
            if l == N_A:
                kb, vb = shared_kv_blocks(x, c_act, kv_norm_g, kv_ada_w, kv_ada_b, w_kv)
            j = l - N_A
            h = modulate(rmsnorm(x, norm_g[l]), shift, scale)
            y = swa_sink_mixer(h, kb, vb, pos_bias, mask, b_w_in[j], b_sinks[j], b_w_out[j])
        x = x + gate[:, None, :] * y
    return rmsnorm(x, final_g)
```

```python
import contextlib
import math
import numpy as np
import concourse.bass as bass
import concourse.mybir as mybir
from concourse.bass_utils import run_bass_kernel_spmd

F32 = mybir.dt.float32
BF16 = mybir.dt.bfloat16
AF = mybir.ActivationFunctionType
ALU = mybir.AluOpType

ENGS = ("pe", "act", "dve", "pool", "sp")

D = 1024
SEQ = 8192
NB = 4
NCORES = 8
TOK = 4096
HALO = 256
NTOK = TOK + HALO
TT = 512
EPS = 1e-6
NEG = -30000.0


class Op:
    __slots__ = ("eng", "fn", "reads", "writes", "chan", "deps", "signal", "ticket", "ndma")

    def __init__(self, eng, fn, reads, writes, chan, ndma):
        self.eng = eng
        self.fn = fn
        self.reads = reads
        self.writes = writes
        self.chan = chan
        self.ndma = ndma
        self.deps = []
        self.signal = False
        self.ticket = None


class Sched:
    def __init__(self):
        self.ops = []
        self.last_writer = {}
        self.readers = {}
        self.chan_last = {}
        self.marks = []

    def mark(self, name):
        self.marks.append((name, len(self.ops)))

    def add(self, eng, fn, reads=(), writes=(), chan=None, ndma=1):
        op = Op(eng, fn, tuple(reads), tuple(writes), chan, ndma)
        deps = []
        for k in op.reads:
            w = self.last_writer.get(k)
            if w is not None:
                deps.append(w)
        for k in op.writes:
            w = self.last_writer.get(k)
            if w is not None:
                deps.append(w)
            deps.extend(self.readers.get(k, ()))
        if chan is not None:
            p = self.chan_last.get(chan)
            if p is not None:
                deps.append(p)
            self.chan_last[chan] = op
        seen = set()
        for d in deps:
            if d is op or id(d) in seen:
                continue
            seen.add(id(d))
            if d.chan is None and d.eng == "pe" and eng == "pe" and chan is None:
                continue
            op.deps.append(d)
            d.signal = True
        for k in op.reads:
            self.readers.setdefault(k, []).append(op)
        for k in op.writes:
            self.last_writer[k] = op
            self.readers[k] = []
        self.ops.append(op)
        return op

    def emit(self, nc):
        cnt = {}
        chans = []
        for op in self.ops:
            if op.chan is not None:
                key = ("chan", op.chan)
                if key not in cnt:
                    chans.append(op.chan)
                cnt[key] = cnt.get(key, 0) + 16 * op.ndma
                op.ticket = cnt[key]
            elif op.signal:
                key = ("eng", op.eng)
                cnt[key] = cnt.get(key, 0) + 1
                op.ticket = cnt[key]
        per_eng = {e: [o for o in self.ops if o.eng == e] for e in ENGS}
        sems = {}
        with contextlib.ExitStack() as st:
            for e in ENGS:
                sems[("eng", e)] = st.enter_context(nc.semaphore("s_" + e))
            for c in chans:
                sems[("chan", c)] = st.enter_context(nc.semaphore("c_" + str(c)))
            block = st.enter_context(nc.Block())

            def body(ename):
                def run(eng):
                    waited = {}
                    for op in per_eng[ename]:
                        for d in op.deps:
                            skey = ("chan", d.chan) if d.chan is not None else ("eng", d.eng)
                            if waited.get(skey, 0) >= d.ticket:
                                continue
                            waited[skey] = d.ticket
                            eng.wait_ge(sems[skey], d.ticket)
                        ins = op.fn(eng)
                        if op.chan is not None:
                            if not isinstance(ins, (list, tuple)):
                                ins = [ins]
                            assert len(ins) == op.ndma, (len(ins), op.ndma)
                            for i_ in ins:
                                i_.then_inc(sems[("chan", op.chan)], 16)
                        elif op.signal:
                            ins.then_inc(sems[("eng", ename)], 1)
                    if ename == "sp":
                        for c in chans:
                            eng.wait_ge(sems[("chan", c)], cnt[("chan", c)])
                return run

            block.tensor(body("pe"))
            block.scalar(body("act"))
            block.vector(body("dve"))
            block.gpsimd(body("pool"))
            block.sync(body("sp"))
        return cnt


def build_nc(depth=4, ntiles=9, raw_out=False, maxops=None, verbose=False):
    nc = bass.Bass("TRN2", target_bir_lowering=False)
    S = Sched()

    def din(name, shape, dt=F32):
        return nc.dram_tensor(name, list(shape), dt, kind="ExternalInput").ap()

    xh = din("xh", [NTOK, D])
    cT = din("cT", [128, 8])
    vecs = din("vecs", [128, 80])
    adab = din("adab", [128, 112])
    ada_w = din("ada_w", [4, D, 3 * D])
    kv_ada_w = din("kv_ada_w", [D, 2 * D])
    a_w_in = din("a_w_in", [2, D, 4096])
    a_w_group = din("a_w_group", [2, 4, 512, 512])
    a_w_out = din("a_w_out", [2, 2048, D])
    w_kv = din("w_kv", [D, 256])
    b_w_in = din("b_w_in", [2, D, 2048])
    b_w_out = din("b_w_out", [2, D, D])
    sinks = din("sinks", [128, 32])
    ebias = din("ebias", [128, 16 * 256])
    valid = din("valid", [128, 1])
    corr = din("corr", [128, 64])
    fgb = din("fgb", [128, D])
    y = nc.dram_tensor("y", [TOK, D], F32, kind="ExternalOutput").ap()
    NSCR = 32
    wscr = nc.dram_tensor("wscr", [NSCR, 128, 8192], BF16, kind="Internal").ap()

    with contextlib.ExitStack() as st:
        def sb(name, shape, dt):
            return st.enter_context(nc.sbuf_tensor(name, list(shape), dt))

        def psb(name):
            return st.enter_context(nc.psum_tensor(name, [128, 512], F32))

        xs = sb("xs", [128, 8, TT], F32)
        NSTG = 4
        stg_in = sb("stg_in", [128, NSTG, D], F32)
        xsq = sb("xsq", [128, 8, TT], BF16)
        h = sb("h", [128, 8, TT], BF16)
        rs = sb("rs", [128, TT], F32)
        tmp = sb("tmp", [128, 2, TT], F32)
        u = sb("u", [128, 2, 4, 16 + TT], F32)
        sA = sb("sA", [128, 16 + TT], F32)
        sB = sb("sB", [128, 16 + TT], F32)
        yb = sb("yb", [128, 2, TT], BF16)
        bufA = sb("bufA", [128, 8, TT], BF16)
        bufB = sb("bufB", [128, 8, TT], BF16)
        m = sb("m", [128, 16, TT], BF16)
        uh = sb("uh", [128, 2, 16, 16], F32)
        kd = sb("kd", [128, 2, 9, 128], BF16)
        vx = sb("vx", [128, 9, 2, 66], BF16)
        E = sb("E", [128, 16, 256], BF16)
        pe_ = sb("pe", [128, 4, 512], BF16)
        pT = sb("pT", [128, 4, 512], BF16)
        otm = sb("otm", [128, 2, 256], BF16)
        den4 = sb("den4", [128, 2, 4], F32)
        identb = sb("identb", [128, 128], BF16)
        W = sb("W", [128, 3, 8192], BF16)
        WG = sb("WG", [128, 8192], BF16)
        ident = sb("ident", [128, 128], F32)
        ones = sb("ones", [128, 128], BF16)
        one1 = sb("one1", [1, 1], F32)
        cT_sb = sb("cT_sb", [128, 8], F32)
        cact = sb("cact", [128, 8], BF16)
        vecs_sb = sb("vecs_sb", [128, 80], F32)
        adab_sb = sb("adab_sb", [128, 112], F32)
        mods = sb("mods", [128, 112], F32)
        gs = sb("gs", [128, 40], F32)
        esink = sb("esink", [128, 32], F32)
        valid_sb = sb("valid_sb", [128, 1], F32)
        corr_sb = sb("corr_sb", [128, 64], F32)
        epsb = sb("epsb", [128, 1], F32)
        rst = sb("rst", [128, 4], F32)
        fgb_sb = sb("fgb_sb", [128, D], F32)
        pb = [psb("pb%d" % i) for i in range(8)]
        row_sb = tmp[0:1, :, :].rearrange("p a b -> p (a b)")
        ROWK = [("tmp", 0), ("tmp", 1)]
        print("sbuf bytes remaining:", nc.sbuf_bytes_remaining)

        V_NG, V_KVG, V_FG, V_AS = 0, 32, 40, 48

        tiles = [(0, HALO)] + [(HALO + i * TT, TT) for i in range(TOK // TT)]
        tiles = tiles[:ntiles]
        gblocks = [(ti_, blk_) for ti_, (t0_, T_) in enumerate(tiles) for blk_ in range(T_ // 128)]
        ldstate = {"n": 0}

        def issue_load():
            n = ldstate["n"]
            if n >= len(gblocks):
                return
            ldstate["n"] += 1
            ti_, blk_ = gblocks[n]
            s_ = n % NSTG
            r0 = tiles[ti_][0] + blk_ * 128
            S.add("sp", lambda e, s_=s_, r0=r0: e.dma_start(out=stg_in[:, s_, :], in_=xh[r0:r0 + 128, :]),
                  writes=[("stg_in", s_)], chan=("ldx", s_))

        for _ in range(NSTG):
            issue_load()
        S.add("sp", lambda e: [e.dma_start(out=cT_sb[:], in_=cT),
                               e.dma_start(out=vecs_sb[:], in_=vecs),
                               e.dma_start(out=adab_sb[:], in_=adab),
                               e.dma_start(out=esink[:], in_=sinks),
                               e.dma_start(out=valid_sb[:], in_=valid),
                               e.dma_start(out=corr_sb[:], in_=corr),
                               e.dma_start(out=fgb_sb[:], in_=fgb)],
              writes=["cT_sb", "vecs", "adab", "esink", "valid", "corr", "fgb"], chan="init", ndma=7)
        S.add("pool", lambda e: e.memset(ident[:], 0.0), writes=["ident"])
        S.add("pool", lambda e: e.affine_select(out=ident[:], in_=ident[:], pattern=[[-1, 128]],
                                                 compare_op=ALU.not_equal, fill=1.0, base=0,
                                                 channel_multiplier=1),
              reads=["ident"], writes=["ident"])
        S.add("pool", lambda e: e.memset(ones[:], 1.0), writes=["ones"])
        S.add("dve", lambda e: e.tensor_copy(out=identb[:], in_=ident[:]), reads=["ident"], writes=["identb"])
        S.add("pool", lambda e: e.memset(one1[:], 1.0), writes=["one1"])
        S.add("pool", lambda e: e.memset(epsb[:], EPS), writes=["epsb"])
        S.add("pool", lambda e: e.memset(uh[:].rearrange("p a b c -> p (a b c)"), 0.0), writes=[("uh", l, c) for l in range(2) for c in range(16)])
        S.add("pool", lambda e: e.memset(vx[:].rearrange("p a b c -> p (a b c)"), 1.0), writes=[("vx", r) for r in range(9)])
        S.add("act", lambda e: e.activation(out=cact[:], in_=cT_sb[:], func=AF.Silu),
              reads=["cT_sb"], writes=["cact"])
        for hf in range(2):
            stg = u[:, hf, :, :].rearrange("p a b -> p (a b)")[:, 0:2048]
            keys = [("u", hf, j) for j in range(4)]
            S.add("sp", lambda e, stg=stg, hf=hf: e.dma_start(out=stg, in_=ebias[:, hf * 2048:(hf + 1) * 2048]),
                  writes=keys, chan="ebias%d" % hf)
            S.add("act", lambda e, stg=stg, hf=hf: e.activation(
                out=E[:, hf * 8:(hf + 1) * 8, :].rearrange("p a b -> p (a b)"), in_=stg, func=AF.Exp),
                reads=keys, writes=["E"])
        S.add("act", lambda e: e.activation(out=esink[:], in_=esink[:], func=AF.Exp),
              reads=["esink"], writes=["esink"])

        wstate = {"scr": {}, "nscr": 0}

        def piece_srcs(name):
            kind = name[0]
            if kind == "ada":
                _, l_, cb = name
                return [(lambda s: v3(s, 8, 1024), kpn(ada_w[l_][:, cb * 1024:(cb + 1) * 1024]))], False
            if kind == "kvada":
                cb = name[1]
                return [(lambda s: v3(s, 8, 1024), kpn(kv_ada_w[:, cb * 1024:(cb + 1) * 1024]))], False
            if kind == "UZ":
                _, l_, g = name
                src = a_w_in[l_].rearrange("(k p) (z g n) -> p k z g n", p=128, z=2, g=4, n=512)
                return [(lambda s: v3(s, 8, 1024)[:, :, 0:512], src[:, :, 0, g, :]),
                        (lambda s: v3(s, 8, 1024)[:, :, 512:1024], src[:, :, 1, g, :])], True
            if kind == "WO":
                _, l_, half = name
                return [(lambda s: v3(s, 8, 1024), kpn(a_w_out[l_][half * 1024:(half + 1) * 1024, :]))], True
            if kind == "KV":
                f = lambda a, b: (lambda s: v3(s[:, 0:8 * 384], 8, 384)[:, :, a:b])
                return [(f(0, 64), kpn(w_kv[:, 0:64])), (f(64, 128), kpn(w_kv[:, 0:64])),
                        (f(128, 192), kpn(w_kv[:, 64:128])), (f(192, 256), kpn(w_kv[:, 64:128])),
                        (f(256, 384), kpn(w_kv[:, 128:256]))], True
            if kind == "BQZ":
                _, jl_, half = name
                return [(lambda s: v3(s, 8, 1024), kpn(b_w_in[jl_][:, half * 1024:(half + 1) * 1024]))], True
            if kind == "BO":
                return [(lambda s: v3(s, 8, 1024), kpn(b_w_out[name[1]]))], True
            if kind == "WG":
                l_ = name[1]
                return [(lambda s: s.rearrange("p (g i n) -> p g i n", g=4, i=4, n=512),
                         a_w_group[l_].rearrange("g (i p) n -> p g i n", p=128))], True
            raise KeyError(name)

        def wb_tile(name):
            kind = name[0]
            if kind in ("UZ", "WO", "WG"):
                return 1 + name[1]
            return 3

        def emit_load(name, slot_ap, key, ti_req=0):
            srcs, keep = piece_srcs(name)
            if keep and name in wstate["scr"]:
                si = wstate["scr"][name]
                S.add("pool", lambda e: e.dma_start(out=slot_ap, in_=wscr[si]),
                      reads=[("scr", si)], writes=[key], chan=("w", key))
            else:
                S.add("pool", lambda e: [e.dma_start(out=f(slot_ap), in_=s_) for f, s_ in srcs],
                      writes=[key], chan=("w", key), ndma=len(srcs))
                if keep and ti_req >= min(wb_tile(name), ntl - 1):
                    si = wstate["nscr"]
                    wstate["nscr"] += 1
                    assert si < NSCR
                    wstate["scr"][name] = si
                    S.add("sp", lambda e: e.dma_start(out=wscr[si], in_=slot_ap),
                          reads=[key], writes=[("scr", si)], chan=("wb", si % 4))

        ntl = min(ntiles, 1 + TOK // TT)
        reqs = []
        wg_uses = []
        req_ti = []
        wg_ti = []
        def mod_names(lm):
            if lm == 4:
                return [("kvada", 0), ("kvada", 1)]
            return [("ada", lm, cb) for cb in range(3)]

        hooks = {}
        if depth >= 2:
            hooks[(0, 0)] = 1
        if depth >= 3:
            hooks[(0, 1)] = 4
            hooks[(1, 0)] = 2
        if depth >= 4:
            hooks[(1, 1)] = 3
        hooks = {k: v for k, v in hooks.items() if k[0] < ntl and k[1] < min(depth, 2)}
        for ti_ in range(ntl):
            n0_, w0_ = len(reqs), len(wg_uses)
            for l_ in range(min(depth, 2)):
                if ti_ == 0 and l_ == 0:
                    reqs += mod_names(0)
                hk_ = mod_names(hooks[(ti_, l_)]) if (ti_, l_) in hooks else []
                hk_ = hk_ + [None] * (3 - len(hk_))
                reqs += [("UZ", l_, 3), ("UZ", l_, 2)]
                for g_, hn_ in zip((1, 0), hk_[:2]):
                    if hn_ is not None:
                        reqs.append(hn_)
                    reqs.append(("UZ", l_, g_))
                if hk_[2] is not None:
                    reqs.append(hk_[2])
                reqs += [("WO", l_, 0), ("WO", l_, 1)]
                wg_uses.append(("WG", l_))
            if depth >= 3:
                reqs.append(("KV",))
                if ti_ >= 1:
                    for jl_ in range(depth - 2):
                        reqs += [("BQZ", jl_, 0), ("BQZ", jl_, 1), ("BO", jl_)]
            req_ti += [ti_] * (len(reqs) - n0_)
            wg_ti += [ti_] * (len(wg_uses) - w0_)
        wq = {"issued": 0, "got": 0, "wg": 0}

        def wq_issue():
            i = wq["issued"]
            if i >= len(reqs):
                return
            wq["issued"] += 1
            sidx = i % 3
            emit_load(reqs[i], W[:, sidx, :], ("W", sidx), req_ti[i])

        def wq_get(name):
            i = wq["got"]
            assert reqs[i] == name, (reqs[i], name)
            assert i < wq["issued"], (i, name)
            wq["got"] += 1
            sidx = i % 3
            return W[:, sidx, :], ("W", sidx)

        def wq_done():
            wq_issue()

        def wg_issue():
            i = wq["wg"]
            if i >= len(wg_uses):
                return
            wq["wg"] += 1
            emit_load(wg_uses[i], WG[:], "WG", wg_ti[i])

        def v3(ap2d, k, n):
            return ap2d.rearrange("p (k n) -> p k n", k=k, n=n)

        def kpn(src2d):
            return src2d.rearrange("(k p) n -> p k n", p=128)

        bank_rr = {"i": 0}

        def mm_bank(lo=1, hi=7):
            n = hi - lo + 1
            b = lo + (bank_rr["i"] % n)
            bank_rr["i"] += 1
            return b

        def mod_piece(name, col0):
            sl, key = wq_get(name)
            w3 = v3(sl, 8, 1024)
            for n in range(2):
                b = mm_bank()
                for k in range(8):
                    S.add("pe", lambda e, b=b, k=k, n=n: e.matmul(
                        pb[b][0:1, :], lhsT=cact[:, k:k + 1], rhs=w3[:, k, n * 512:(n + 1) * 512],
                        start=(k == 0), stop=(k == 7)),
                        reads=[key, "cact"], writes=[("pb", b)])
                S.add("dve", lambda e, b=b, n=n: e.tensor_copy(out=row_sb[0:1, n * 512:(n + 1) * 512], in_=pb[b][0:1, :]),
                      reads=[("pb", b)], writes=ROWK)
            wq_done()
            b = mm_bank()
            for j in range(8):
                S.add("pe", lambda e, b=b, j=j: e.matmul(
                    pb[b][:, j:j + 1], lhsT=row_sb[0:1, j * 128:(j + 1) * 128], rhs=one1[0:1, 0:1],
                    start=True, stop=True),
                    reads=ROWK + ["one1"], writes=[("pb", b)])
            S.add("dve", lambda e, b=b: e.tensor_tensor(out=mods[:, col0:col0 + 8], in0=pb[b][:, 0:8],
                                                        in1=adab_sb[:, col0:col0 + 8], op=ALU.add),
                  reads=[("pb", b), "adab"], writes=["mods"])

        def mod_one(name):
            if name[0] == "ada":
                mod_piece(name, name[1] * 24 + name[2] * 8)
            else:
                mod_piece(name, 96 + name[1] * 8)

        def mods_finish(l):
            sc = (l * 24 + 8) if l < 4 else 104
            ng = (V_NG + l * 8) if l < 4 else V_KVG
            S.add("dve", lambda e: e.scalar_tensor_tensor(
                out=gs[:, l * 8:(l + 1) * 8], in0=mods[:, sc:sc + 8], scalar=1.0, in1=vecs_sb[:, ng:ng + 8],
                op0=ALU.add, op1=ALU.mult),
                reads=["mods", "vecs"], writes=["gs"])

        def mods_layer(l):
            for nm in mod_names(l):
                mod_one(nm)
            mods_finish(l)

        S.mark("mods")
        for _ in range(3):
            wq_issue()
        wg_issue()

        def shift_col(l):
            return (l * 24) if l < 4 else 96

        def gate_col(l):
            return l * 24 + 16

        def hvs(t0, Th):
            return list(range(t0 // 256, (t0 + Th + 255) // 256))

        def halves(T):
            return [(0, T)] if T <= 256 else [(0, 256), (256, 256)]

        def XK(c, t0, Th):
            return [("xs", c, hf) for hf in hvs(t0, Th)]

        def HK(c, t0, Th):
            return [("h", c, hf) for hf in hvs(t0, Th)]

        def RK(t0, Th):
            return [("rs", hf) for hf in hvs(t0, Th)]

        def QK(c, t0, Th):
            return [("xsq", c, hf) for hf in hvs(t0, Th)]

        def squares(t0, Th):
            for c in range(8):
                S.add("act", lambda e, c=c: e.activation(out=xsq[:, c, t0:t0 + Th], in_=xs[:, c, t0:t0 + Th], func=AF.Square),
                      reads=XK(c, t0, Th), writes=QK(c, t0, Th))

        def ssum(t0, Th):
            b = 0
            for c in range(8):
                S.add("pe", lambda e, c=c: e.matmul(pb[b][:, t0:t0 + Th], lhsT=ones[:], rhs=xsq[:, c, t0:t0 + Th],
                                                     start=(c == 0), stop=(c == 7)),
                      reads=QK(c, t0, Th) + ["ones"], writes=[("pb", b)])
            S.add("act", lambda e: e.activation(out=rs[:, t0:t0 + Th], in_=pb[b][:, t0:t0 + Th], func=AF.Ln,
                                                bias=epsb[:], scale=1.0 / D),
                  reads=[("pb", b), "epsb"], writes=RK(t0, Th))
            S.add("act", lambda e: e.activation(out=rs[:, t0:t0 + Th], in_=rs[:, t0:t0 + Th], func=AF.Exp, scale=-0.5),
                  reads=RK(t0, Th), writes=RK(t0, Th))

        def front_stats(T):
            squares(0, T)
            ssum(0, T)

        def keep_warm(n=8, extra=()):
            for _ in range(n):
                S.add("pe", lambda e: e.matmul(pb[0][:, :].rearrange("p (a b) -> p a b", a=2), lhsT=ones[:],
                                               rhs=E[:, 0:2, :], start=True, stop=True),
                      reads=["E", "ones"] + list(extra), writes=[("pb", 0)])

        def fronth(t0, Th, gcol, scol):
            for c in range(8):
                s_ = c % 2
                S.add("dve", lambda e, c=c, s_=s_: e.scalar_tensor_tensor(
                    out=tmp[:, s_, :Th], in0=xs[:, c, t0:t0 + Th], scalar=gs[:, gcol + c:gcol + c + 1],
                    in1=rs[:, t0:t0 + Th], op0=ALU.mult, op1=ALU.mult),
                    reads=XK(c, t0, Th) + RK(t0, Th) + ["gs"], writes=[("tmp", s_)])
                S.add("act", lambda e, c=c, s_=s_: e.activation(
                    out=h[:, c, t0:t0 + Th], in_=tmp[:, s_, :Th], func=AF.Identity, bias=mods[:, scol + c:scol + c + 1]),
                    reads=[("tmp", s_), "mods"], writes=HK(c, t0, Th))

        def front_h(T, gcol, scol, warm=False):
            fronth(0, T, gcol, scol)

        pend = {"fn": None}

        def flush_pending():
            f = pend["fn"]
            pend["fn"] = None
            if f is not None:
                f()

        def boundary(T, out_fn, nxt):
            hv = halves(T)
            if len(hv) == 1 or nxt is None:
                for (t0, Th) in hv:
                    out_fn(t0, Th, 0, 8)
                if nxt is not None:
                    for (t0, Th) in hv:
                        squares(t0, Th)
                        ssum(t0, Th)
                        fronth(t0, Th, nxt[0], nxt[1])
                return
            (a0_, ah), (b0_, bh) = hv
            out_fn(a0_, ah, 0, 8)
            squares(a0_, ah)
            out_fn(b0_, bh, 0, 2)
            ssum(a0_, ah)
            out_fn(b0_, bh, 2, 4)
            fronth(a0_, ah, nxt[0], nxt[1])
            out_fn(b0_, bh, 4, 8)
            squares(b0_, bh)

            def _pb():
                ssum(b0_, bh)
                fronth(b0_, bh, nxt[0], nxt[1])
            pend["fn"] = _pb

        hkeys = [("h", c) for c in range(8)]

        def layer_a(l, ti, T, nxt):
            wg_key = "WG"
            wg4 = WG[:].rearrange("p (g i n) -> p g i n", g=4, i=4, n=512)
            hv = halves(T)

            def uz_mm(g, w3, key, t0, Th, first, last):
                us = g % 2
                for j in range(4):
                    b = mm_bank()
                    for k in range(8):
                        S.add("pe", lambda e, b=b, k=k, j=j: e.matmul(
                            pb[b][:, :Th], lhsT=w3[:, k, j * 128:(j + 1) * 128], rhs=h[:, k, t0:t0 + Th],
                            start=(k == 0), stop=(k == 7)),
                            reads=[key] + HK(k, t0, Th), writes=[("pb", b)])
                    ukey = ("u", us, j)
                    if first:
                        S.add("act", lambda e, j=j: e.activation(out=u[:, us, j, 0:16], in_=uh[:, l, g * 4 + j, :], func=AF.Copy),
                              reads=[("uh", l, g * 4 + j)], writes=[ukey])
                    if ti == 0:
                        S.add("act", lambda e, b=b, j=j: e.activation(
                            out=u[:, us, j, 16 + t0:16 + t0 + Th], in_=pb[b][:, :Th], func=AF.Identity, scale=valid_sb[:, 0:1]),
                            reads=[("pb", b), "valid"], writes=[ukey])
                    else:
                        S.add("act", lambda e, b=b, j=j: e.activation(
                            out=u[:, us, j, 16 + t0:16 + t0 + Th], in_=pb[b][:, :Th], func=AF.Copy),
                            reads=[("pb", b)], writes=[ukey])
                    if last:
                        if ti == 0:
                            S.add("act", lambda e, b=b, j=j: e.activation(
                                out=uh[:, l, g * 4 + j, :], in_=pb[b][:, Th - 16:Th], func=AF.Identity, scale=valid_sb[:, 0:1]),
                                reads=[("pb", b), "valid"], writes=[("uh", l, g * 4 + j)])
                        else:
                            S.add("act", lambda e, b=b, j=j: e.activation(
                                out=uh[:, l, g * 4 + j, :], in_=pb[b][:, Th - 16:Th], func=AF.Copy),
                                reads=[("pb", b)], writes=[("uh", l, g * 4 + j)])
                for j in range(4):
                    b = mm_bank()
                    for k in range(8):
                        S.add("pe", lambda e, b=b, k=k, j=j: e.matmul(
                            pb[b][:, :Th], lhsT=w3[:, k, 512 + j * 128:512 + (j + 1) * 128], rhs=h[:, k, t0:t0 + Th],
                            start=(k == 0), stop=(k == 7)),
                            reads=[key] + HK(k, t0, Th), writes=[("pb", b)])
                    S.add("act", lambda e, b=b, j=j: e.activation(
                        out=bufB[:, us * 4 + j, t0:t0 + Th], in_=pb[b][:, :Th], func=AF.Silu),
                        reads=[("pb", b)], writes=[("bufB", us * 4 + j)])

            def uz(g):
                sl, key = wq_get(("UZ", l, g))
                uz_mm(g, v3(sl, 8, 1024), key, 0, T, True, True)
                wq_done()

            def uz_split(gs_):
                ps = []
                for g in gs_:
                    sl, key = wq_get(("UZ", l, g))
                    ps.append((g, v3(sl, 8, 1024), key))
                for hi, (t0, Th) in enumerate(hv):
                    for gi, (g, w3, key) in enumerate(ps):
                        uz_mm(g, w3, key, t0, Th, hi == 0, hi == len(hv) - 1)
                        if hi == 0 and gi == 0:
                            flush_pending()
                flush_pending()
                for _ in gs_:
                    wq_done()

            def pool_g(g):
                us = g % 2
                w = 2 ** (g + 1)
                for j in range(4):
                    ukey = ("u", us, j)
                    src, skey = u[:, us, j, :], ukey
                    eng_ = "dve"
                    bufs = [(sA, "sA"), (sB, "sB")]
                    lo = 0
                    for k in range(g + 1):
                        d = 2 ** k
                        dst, dkey = bufs[k % 2]
                        S.add(eng_, lambda e, src=src, dst=dst, lo=lo, d=d: e.tensor_tensor(
                            out=dst[:, lo + d:16 + T], in0=src[:, lo + d:16 + T], in1=src[:, lo:16 + T - d], op=ALU.add),
                            reads=[skey], writes=[dkey])
                        src, skey = dst, dkey
                        lo += d
                    if ti == 1:
                        S.add(eng_, lambda e, src=src: e.tensor_tensor(
                            out=src[:, 16:32], in0=src[:, 16:32], in1=corr_sb[:, g * 16:(g + 1) * 16], op=ALU.mult),
                            reads=[skey, "corr"], writes=[skey])
                    S.add("dve", lambda e, src=src, j=j: e.scalar_tensor_tensor(
                        out=bufA[:, us * 4 + j, :T], in0=src[:, 16:16 + T], scalar=1.0 / w, in1=u[:, us, j, 16:16 + T],
                        op0=ALU.mult, op1=ALU.subtract),
                        reads=[skey, ukey], writes=[("bufA", us * 4 + j)])

            def ymat(g):
                us = g % 2
                for j in range(4):
                    b = mm_bank()
                    for i in range(4):
                        S.add("pe", lambda e, b=b, i=i, j=j: e.matmul(
                            pb[b][:, :T], lhsT=wg4[:, g, i, j * 128:(j + 1) * 128], rhs=bufA[:, us * 4 + i, :T],
                            start=(i == 0), stop=(i == 3)),
                            reads=[wg_key, ("bufA", us * 4 + i)], writes=[("pb", b)])
                    kk = g * 4 + j
                    ys = kk % 2
                    S.add("act", lambda e, b=b, kk=kk, ys=ys: e.activation(
                        out=yb[:, ys, :T], in_=pb[b][:, :T], func=AF.Identity,
                        scale=vecs_sb[:, V_AS + l * 16 + kk:V_AS + l * 16 + kk + 1]),
                        reads=[("pb", b), "vecs"], writes=[("yb", ys)])
                    S.add("dve", lambda e, j=j, kk=kk, ys=ys: e.tensor_tensor(
                        out=m[:, kk, :T], in0=yb[:, ys, :T], in1=bufB[:, us * 4 + j, :T], op=ALU.mult),
                        reads=[("yb", ys), ("bufB", us * 4 + j)], writes=[("m", kk)])

            hk = list(mod_names(hooks[(ti, l)])) if (ti, l) in hooks else []
            hk = hk + [None] * (3 - len(hk))
            uz_split((3, 2))
            pool_g(3)
            pool_g(2)
            ymat(3)
            if hk[0] is not None:
                mod_one(hk[0])
            uz(1)
            pool_g(1)
            ymat(2)
            if hk[1] is not None:
                mod_one(hk[1])
            uz(0)
            pool_g(0)
            ymat(1)
            ymat(0)
            if hk[2] is not None:
                mod_one(hk[2])
            if (ti, l) in hooks:
                mods_finish(hooks[(ti, l)])
            wg_issue()
            wo = []
            for half in range(2):
                sl, key = wq_get(("WO", l, half))
                wo.append((v3(sl, 8, 1024), key))
            gc = gate_col(l)

            def out_fn(t0, Th, c_lo, c_hi):
                for c in range(c_lo, c_hi):
                    b = mm_bank()
                    for k in range(16):
                        w3, key = wo[k // 8]
                        S.add("pe", lambda e, b=b, k=k, c=c, w3=w3: e.matmul(
                            pb[b][:, :Th], lhsT=w3[:, k % 8, c * 128:(c + 1) * 128], rhs=m[:, k, t0:t0 + Th],
                            start=(k == 0), stop=(k == 15)),
                            reads=[key, ("m", k)], writes=[("pb", b)])
                    S.add("dve", lambda e, b=b, c=c: e.scalar_tensor_tensor(
                        out=xs[:, c, t0:t0 + Th], in0=pb[b][:, :Th], scalar=mods[:, gc + c:gc + c + 1],
                        in1=xs[:, c, t0:t0 + Th], op0=ALU.mult, op1=ALU.add),
                        reads=[("pb", b), "mods"] + XK(c, t0, Th), writes=XK(c, t0, Th))

            boundary(T, out_fn, nxt)
            wq_done()
            wq_done()

        def ring_of(gb):
            return 8 if gb < 0 else gb % 8

        def kv_part(ti, T):
            flush_pending()
            sl, key = wq_get(("KV",))
            w3 = v3(sl[:, 0:8 * 384], 8, 384)
            if ti == 0:
                blocks = [(-1, 1)]
            else:
                blocks = [(4 * (ti - 1) + i, i) for i in range(T // 128)]
            lb0 = blocks[0][1]
            nb = len(blocks)
            r0 = ring_of(blocks[0][0])
            rkeys = [("kd", ring_of(gb)) for gb, _ in blocks]
            vkeys = [("vx", ring_of(gb)) for gb, _ in blocks]
            for g in range(2):
                b = mm_bank()
                for k in range(8):
                    S.add("pe", lambda e, b=b, k=k, g=g: e.matmul(
                        pb[b][:, :nb * 128], lhsT=w3[:, k, g * 128:(g + 1) * 128],
                        rhs=h[:, k, lb0 * 128:(lb0 + nb) * 128], start=(k == 0), stop=(k == 7)),
                        reads=[key] + HK(k, lb0 * 128, nb * 128), writes=[("pb", b)])
                S.add("dve", lambda e, b=b, g=g: e.tensor_copy(
                    out=kd[:, g, r0:r0 + nb, :], in_=pb[b][:, :nb * 128].rearrange("p (a b) -> p a b", a=nb)),
                    reads=[("pb", b)], writes=rkeys)
            b = mm_bank()
            for i, (gb, lb) in enumerate(blocks):
                for k in range(8):
                    S.add("pe", lambda e, b=b, k=k, i=i, lb=lb: e.matmul(
                        pb[b][:, i * 128:(i + 1) * 128], lhsT=h[:, k, lb * 128:(lb + 1) * 128],
                        rhs=w3[:, k, 256:384], start=(k == 0), stop=(k == 7)),
                        reads=[key] + HK(k, lb * 128, 128), writes=[("pb", b)])
            wq_done()
            S.add("dve", lambda e, b=b: e.tensor_copy(
                out=vx[:, r0:r0 + nb, :, 0:64],
                in_=pb[b][:, :nb * 128].rearrange("p (a g d) -> p a g d", a=nb, g=2)),
                reads=[("pb", b)], writes=vkeys)
            if ti == 0:
                S.add("dve", lambda e: e.tensor_scalar(
                    out=vx[:, 8, :, :].rearrange("p a b -> p (a b)"),
                    in0=vx[:, 8, :, :].rearrange("p a b -> p (a b)"),
                    scalar1=valid_sb[:, 0:1], scalar2=None, op0=ALU.mult),
                    reads=[("vx", 8), "valid"], writes=[("vx", 8)])

        def layer_b(l, ti, T, nxt, own_front):
            jl = l - 2
            hv = halves(T)
            if own_front:
                for (t0, Th) in hv:
                    fronth(t0, Th, l * 8, shift_col(l))
            qz = []
            for half in range(2):
                sl, key = wq_get(("BQZ", jl, half))
                qz.append((v3(sl, 8, 1024), key))
            for hi_, (t0, Th) in enumerate(hv):
                for half in range(2):
                    if hi_ == 0 and half == 1:
                        flush_pending()
                    w3, key = qz[half]
                    for c in range(8):
                        b = mm_bank()
                        for k in range(8):
                            S.add("pe", lambda e, b=b, k=k, c=c, w3=w3, t0=t0, Th=Th: e.matmul(
                                pb[b][:, :Th], lhsT=w3[:, k, c * 128:(c + 1) * 128], rhs=h[:, k, t0:t0 + Th],
                                start=(k == 0), stop=(k == 7)),
                                reads=[key] + HK(k, t0, Th), writes=[("pb", b)])
                        if half == 0:
                            S.add("act", lambda e, b=b, c=c, t0=t0, Th=Th: e.activation(
                                out=bufA[:, c, t0:t0 + Th], in_=pb[b][:, :Th], func=AF.Copy),
                                reads=[("pb", b)], writes=[("bufA", c)])
                        else:
                            S.add("act", lambda e, b=b, c=c, t0=t0, Th=Th: e.activation(
                                out=bufB[:, c, t0:t0 + Th], in_=pb[b][:, :Th], func=AF.Silu),
                                reads=[("pb", b)], writes=[("bufB", c)])
            flush_pending()
            wq_done()
            wq_done()
            units = [(blk, q) for blk in range(T // 128) for q in range(4)]
            sbank = {"i": 0}

            def scores(ui):
                blk, q = units[ui]
                gblk = 4 * (ti - 1) + blk
                g = q // 2
                banks = []
                for par in range(2):
                    banks.append(1 + (sbank["i"] % 4))
                    sbank["i"] += 1
                for hh in range(2):
                    c = 2 * q + hh
                    for kb in range(2):
                        rg = ring_of(gblk - 1 + kb)
                        for par in range(2):
                            b = banks[par]
                            r0 = par * 64
                            S.add("pe", lambda e, b=b, hh=hh, kb=kb, rg=rg, c=c, r0=r0: e.matmul(
                                pb[b][:, (hh * 2 + kb) * 128:(hh * 2 + kb + 1) * 128],
                                lhsT=kd[r0:r0 + 64, g, rg, :], rhs=bufA[r0:r0 + 64, c, blk * 128:(blk + 1) * 128],
                                start=True, stop=True),
                                reads=[("kd", rg), ("bufA", c)], writes=[("pb", b)])
                return banks

            def stage_x(ui, banks):
                blk, q = units[ui]
                for par in range(2):
                    b = banks[par]
                    ps_ = (ui * 2 + par) % 4
                    S.add("act", lambda e, b=b, ps_=ps_: e.activation(out=pe_[:, ps_, :], in_=pb[b][:, :], func=AF.Exp,
                                                                      scale=0.125),
                          reads=[("pb", b)], writes=[("pe", ps_)])
                    h0 = 4 * q + par
                    S.add("dve" if par == 0 else "pool", lambda e, ps_=ps_, h0=h0: e.tensor_tensor(
                        out=pT[:, ps_, :].rearrange("p (a b) -> p a b", a=2),
                        in0=pe_[:, ps_, :].rearrange("p (a b) -> p a b", a=2),
                        in1=E[:, h0:h0 + 3:2, :], op=ALU.mult),
                        reads=[("pe", ps_), "E"], writes=[("pT", ps_)])

            def stage_v(ui):
                blk, q = units[ui]
                gblk = 4 * (ti - 1) + blk
                g = q // 2
                ob = 5 + (ui % 2)
                sl_ = ui % 2
                for par in range(2):
                    ps_ = (ui * 2 + par) % 4
                    p4 = pT[:, ps_, :].rearrange("p (a k b) -> p a k b", a=2, k=2)
                    for hh in range(2):
                        hl = 2 * hh + par
                        for kb in range(2):
                            rg = ring_of(gblk - 1 + kb)
                            S.add("pe", lambda e, kb=kb, rg=rg, hh=hh, hl=hl, p4=p4: e.matmul(
                                pb[ob][:, hl * 128:hl * 128 + 65], lhsT=p4[:, hh, kb, :], rhs=vx[:, rg, g, 0:65],
                                start=(kb == 0), stop=(kb == 1)),
                                reads=[("vx", rg), ("pT", ps_)], writes=[("pb", ob)])
                ec = jl * 16 + 4 * q
                o4 = pb[ob][:, :].rearrange("p (a b) -> p a b", a=4)
                S.add("dve", lambda e: e.tensor_tensor(
                    out=den4[:, sl_, :], in0=o4[:, :, 64], in1=esink[:, ec:ec + 4], op=ALU.add),
                    reads=[("pb", ob), "esink"], writes=[("den4", sl_)])
                S.add("dve", lambda e: e.reciprocal(out=den4[:, sl_, :], in_=den4[:, sl_, :]),
                      reads=[("den4", sl_)], writes=[("den4", sl_)])
                S.add("dve", lambda e: e.tensor_tensor(
                    out=otm[:, sl_, :].rearrange("p (a b) -> p a b", a=4), in0=o4[:, :, 0:64],
                    in1=den4[:, sl_, :].unsqueeze(2).to_broadcast([128, 4, 64]), op=ALU.mult),
                    reads=[("pb", ob), ("den4", sl_)], writes=[("otm", sl_)])

            def stage_t(ui):
                blk, q = units[ui]
                sl_ = ui % 2
                tb = 7 if ui % 2 == 0 else 0
                tbv = pb[tb][:, :].bitcast(BF16)
                for c2 in range(2):
                    S.add("pe", lambda e, c2=c2: e.transpose(
                        out=tbv[:, c2 * 128:(c2 + 1) * 128], in_=otm[:, sl_, c2 * 128:(c2 + 1) * 128],
                        identity=identb[:]),
                        reads=[("otm", sl_), "identb"], writes=[("pb", tb)])
                S.add("dve", lambda e: e.tensor_tensor(
                    out=m[:, 2 * q:2 * q + 2, blk * 128:(blk + 1) * 128],
                    in0=tbv[:, 0:256].rearrange("p (a b) -> p a b", a=2),
                    in1=bufB[:, 2 * q:2 * q + 2, blk * 128:(blk + 1) * 128], op=ALU.mult),
                    reads=[("pb", tb), ("bufB", 2 * q), ("bufB", 2 * q + 1)],
                    writes=[("m", 2 * q), ("m", 2 * q + 1)])

            sl, key = wq_get(("BO", jl))
            nu = len(units)
            sb_ = {0: scores(0)}
            if nu > 1:
                sb_[1] = scores(1)
            stage_x(0, sb_[0])
            for ui in range(nu):
                if ui + 2 < nu:
                    sb_[ui + 2] = scores(ui + 2)
                if ui + 1 < nu:
                    stage_x(ui + 1, sb_[ui + 1])
                stage_v(ui)
                if ui >= 1:
                    stage_t(ui - 1)
            stage_t(nu - 1)
            w3 = v3(sl, 8, 1024)
            gc = gate_col(l)

            def out_fn(t0, Th, c_lo, c_hi):
                for c in range(c_lo, c_hi):
                    b = mm_bank()
                    for k in range(8):
                        S.add("pe", lambda e, b=b, k=k, c=c: e.matmul(
                            pb[b][:, :Th], lhsT=w3[:, k, c * 128:(c + 1) * 128], rhs=m[:, k, t0:t0 + Th],
                            start=(k == 0), stop=(k == 7)),
                            reads=[key, ("m", k)], writes=[("pb", b)])
                    S.add("dve", lambda e, b=b, c=c: e.scalar_tensor_tensor(
                        out=xs[:, c, t0:t0 + Th], in0=pb[b][:, :Th], scalar=mods[:, gc + c:gc + c + 1],
                        in1=xs[:, c, t0:t0 + Th], op0=ALU.mult, op1=ALU.add),
                        reads=[("pb", b), "mods"] + XK(c, t0, Th), writes=XK(c, t0, Th))

            boundary(T, out_fn, nxt)
            wq_done()

        stcount = {"i": 0}
        S.mark("tiles")
        gbi = 0
        for ti, (t0, T) in enumerate(tiles):
            S.mark("tile%d" % ti)
            nl = depth
            hv = halves(T)
            spec = lambda l_: (l_ * 8, shift_col(l_)) if l_ < 4 else (32, shift_col(4))
            if ti == 0 and nl >= 1:
                mods_layer(0)

            def in_transposes(blk):
                global_slot = in_transposes.gbi % NSTG
                in_transposes.gbi += 1
                s_ = global_slot
                for hc in range(2):
                    b = mm_bank()
                    for cc in range(4):
                        c = hc * 4 + cc
                        S.add("pe", lambda e, b=b, cc=cc, c=c, s_=s_: e.transpose(
                            out=pb[b][:, cc * 128:(cc + 1) * 128], in_=stg_in[:, s_, c * 128:(c + 1) * 128],
                            identity=ident[:]),
                            reads=[("stg_in", s_), "ident"], writes=[("pb", b)])
                    S.add("act", lambda e, b=b, hc=hc, blk=blk: e.activation(
                        out=xs[:, hc * 4:(hc + 1) * 4, blk * 128:(blk + 1) * 128],
                        in_=pb[b][:, :].rearrange("p (a b) -> p a b", a=4), func=AF.Copy),
                        reads=[("pb", b)],
                        writes=[k_ for cc in range(4) for k_ in XK(hc * 4 + cc, blk * 128, 128)])
                issue_load()
            in_transposes.gbi = gbi

            first = spec(0) if nl >= 1 else None
            nb_h = 2
            for hi, (t0h, Th) in enumerate(hv):
                for blk in range(t0h // 128, (t0h + Th) // 128):
                    in_transposes(blk)
                if first is not None:
                    squares(t0h, Th)
                    if hi == 0 or len(hv) == 1:
                        ssum(t0h, Th)
                        fronth(t0h, Th, first[0], first[1])
                    else:
                        def _pb0(t0h=t0h, Th=Th):
                            ssum(t0h, Th)
                            fronth(t0h, Th, first[0], first[1])
                        pend["fn"] = _pb0
            gbi = in_transposes.gbi
            S.mark("tile%d-layers" % ti)
            if nl >= 1:
                nxt = spec(1) if nl >= 2 else (spec(4) if nl >= 3 else None)
                layer_a(0, ti, T, nxt)
            if nl >= 2:
                layer_a(1, ti, T, spec(4) if nl >= 3 else None)
            if nl >= 3:
                kv_part(ti, T)
                if ti >= 1:
                    layer_b(2, ti, T, spec(3) if nl >= 4 else None, True)
                    if nl >= 4:
                        layer_b(3, ti, T, None, False)
            if ti == 0:
                continue
            flush_pending()
            nblk = T // 128
            if not raw_out:
                for c in range(8):
                    S.add("act", lambda e, c=c: e.activation(out=xsq[:, c, :T], in_=xs[:, c, :T], func=AF.Square),
                          reads=XK(c, 0, T), writes=QK(c, 0, T))
                for blk in range(nblk):
                    for c in range(8):
                        S.add("pe", lambda e, c=c, blk=blk: e.matmul(
                            pb[0][:, blk:blk + 1], lhsT=xsq[:, c, blk * 128:(blk + 1) * 128], rhs=ones[:, 0:1],
                            start=(c == 0), stop=(c == 7)),
                            reads=QK(c, 0, T) + ["ones"], writes=[("pb", 0)])
                S.add("act", lambda e: e.activation(out=rst[:, 0:nblk], in_=pb[0][:, 0:nblk], func=AF.Ln,
                                                    bias=epsb[:], scale=1.0 / D),
                      reads=[("pb", 0), "epsb"], writes=["rst"])
                S.add("act", lambda e: e.activation(out=rst[:, 0:nblk], in_=rst[:, 0:nblk], func=AF.Exp, scale=-0.5),
                      reads=["rst"], writes=["rst"])
            for blk in range(nblk):
                so_ap = m[:, 4 * blk:4 * blk + 4, :].rearrange("p a b -> p (a b)").bitcast(F32)
                so_keys = [("m", 4 * blk + i_) for i_ in range(4)]
                for hc in range(2):
                    b = mm_bank(1, 7)
                    for cc in range(4):
                        c = hc * 4 + cc
                        S.add("pe", lambda e, b=b, cc=cc, c=c, blk=blk: e.transpose(
                            out=pb[b][:, cc * 128:(cc + 1) * 128], in_=xs[:, c, blk * 128:(blk + 1) * 128],
                            identity=ident[:]),
                            reads=XK(c, 0, T) + ["ident"], writes=[("pb", b)])
                    if raw_out:
                        S.add("dve", lambda e, b=b, hc=hc, so_ap=so_ap: e.tensor_copy(
                            out=so_ap[:, hc * 512:(hc + 1) * 512], in_=pb[b][:, :]),
                            reads=[("pb", b)], writes=so_keys)
                    else:
                        S.add("dve", lambda e, b=b, hc=hc, so_ap=so_ap, blk=blk: e.scalar_tensor_tensor(
                            out=so_ap[:, hc * 512:(hc + 1) * 512], in0=pb[b][:, :],
                            scalar=rst[:, blk:blk + 1], in1=fgb_sb[:, hc * 512:(hc + 1) * 512],
                            op0=ALU.mult, op1=ALU.mult),
                            reads=[("pb", b), "rst", "fgb"], writes=so_keys)
                r0 = t0 - HALO + blk * 128
                S.add("sp", lambda e, so_ap=so_ap, r0=r0: e.dma_start(out=y[r0:r0 + 128, :], in_=so_ap),
                      reads=so_keys, chan=("sty", blk % 2))
        if verbose:
            for name, n in S.marks:
                print("mark", name, n)
            print("total ops", len(S.ops))
        if maxops is not None:
            S.ops = S.ops[:maxops]
        S.emit(nc)
    return nc


def _t5_buckets():
    i = np.arange(128)[:, None]
    j = np.arange(256)[None, :]
    n = np.maximum(i + 128 - j, 0)
    max_exact = 16
    large = max_exact + (np.log(np.maximum(n, 1) / max_exact) / math.log(128 / max_exact)
                         * (32 - max_exact)).astype(np.int32)
    large = np.minimum(large, 31)
    return np.where(n < max_exact, n, large).astype(np.int32)


def fm(v):
    v = np.asarray(v, np.float32).reshape(-1, 128)
    return np.ascontiguousarray(v.T)


def prep_inputs(x, c, norm_g, ada_w, ada_b, a_w_in, a_w_group, a_scale, a_w_out,
                kv_norm_g, kv_ada_w, kv_ada_b, w_kv, b_w_in, b_sinks, b_w_out,
                rel_bias, final_g):
    f32 = lambda a: np.ascontiguousarray(np.asarray(a, np.float32))
    x = f32(x)
    bk = _t5_buckets()
    rb = f32(rel_bias)
    pos = rb[bk]
    i = np.arange(128)[:, None]
    jj = np.arange(256)[None, :]
    rel = i + 128 - jj
    band = (rel >= 0) & (rel < 128)
    tab = np.where(band[:, :, None], pos, np.float32(NEG)).astype(np.float32)
    tab = tab.reshape(128, 2, 128, 16).transpose(2, 3, 1, 0)
    ebias = np.ascontiguousarray(tab.reshape(128, 16 * 256))
    vecs = np.concatenate([fm(norm_g[l]) for l in range(4)] + [fm(kv_norm_g), fm(final_g)]
                          + [fm(a_scale[l]) for l in range(2)], axis=1)
    adab = np.concatenate([fm(ada_b[l]) for l in range(4)] + [fm(kv_ada_b)], axis=1)
    sinks = np.ascontiguousarray(np.broadcast_to(f32(b_sinks).reshape(1, 32), (128, 32)))
    shared = {
        "vecs": f32(vecs), "adab": f32(adab), "ada_w": f32(ada_w), "kv_ada_w": f32(kv_ada_w),
        "a_w_in": f32(a_w_in), "a_w_group": f32(a_w_group), "a_w_out": f32(a_w_out),
        "w_kv": f32(w_kv), "b_w_in": f32(b_w_in), "b_w_out": f32(b_w_out),
        "sinks": sinks, "ebias": ebias,
        "fgb": np.ascontiguousarray(np.broadcast_to(f32(final_g).reshape(1, D), (128, D))),
    }
    in_maps = []
    for core in range(NCORES):
        b, half = core // 2, core % 2
        start = half * TOK
        xh = np.zeros((NTOK, D), np.float32)
        if half == 0:
            xh[HALO:] = x[b, 0:TOK]
        else:
            xh[:] = x[b, start - HALO:start + TOK]
        corr = np.ones((4, 16), np.float32)
        if half == 0:
            t = np.arange(16)
            for g in range(4):
                w = 2 ** (g + 1)
                corr[g] = w / np.minimum(t + 1, w)
        mp = dict(shared)
        mp["xh"] = xh
        mp["cT"] = fm(c[b])
        mp["valid"] = np.full((128, 1), 0.0 if half == 0 else 1.0, np.float32)
        mp["corr"] = np.ascontiguousarray(np.broadcast_to(corr.reshape(1, 64), (128, 64)))
        in_maps.append(mp)
    return in_maps


_NC_CACHE = {}


def kernel(**inputs):
    in_maps = prep_inputs(**inputs)
    if "nc" not in _NC_CACHE:
        _NC_CACHE["nc"] = build_nc()
    nc = _NC_CACHE["nc"]
    res = run_bass_kernel_spmd(nc, in_maps, core_ids=list(range(NCORES)))
    out = np.empty((NB, SEQ, D), np.float32)
    for core in range(NCORES):
        b, half = core // 2, core % 2
        out[b, half * TOK:(half + 1) * TOK] = res.results[core]["y"]
    return out
```

```python
import contextlib
import math
import numpy as np
import concourse.bass as bass
import concourse.mybir as mybir
from concourse.bass_utils import run_bass_kernel_spmd

F32 = mybir.dt.float32
BF16 = mybir.dt.bfloat16
AF = mybir.ActivationFunctionType
ALU = mybir.AluOpType

ENGS = ("pe", "act", "dve", "pool", "sp")

D = 1024
SEQ = 8192
NB = 4
NCORES = 8
TOK = 4096
HALO = 256
NTOK = TOK + HALO
TT = 512
EPS = 1e-6
NEG = -30000.0


class Op:
    __slots__ = ("eng", "fn", "reads", "writes", "chan", "deps", "signal", "ticket", "ndma")

    def __init__(self, eng, fn, reads, writes, chan, ndma):
        self.eng = eng
        self.fn = fn
        self.reads = reads
        self.writes = writes
        self.chan = chan
        self.ndma = ndma
        self.deps = []
        self.signal = False
        self.ticket = None


class Sched:
    def __init__(self):
        self.ops = []
        self.last_writer = {}
        self.readers = {}
        self.chan_last = {}
        self.marks = []

    def mark(self, name):
        self.marks.append((name, len(self.ops)))

    def add(self, eng, fn, reads=(), writes=(), chan=None, ndma=1):
        op = Op(eng, fn, tuple(reads), tuple(writes), chan, ndma)
        deps = []
        for k in op.reads:
            w = self.last_writer.get(k)
            if w is not None:
                deps.append(w)
        for k in op.writes:
            w = self.last_writer.get(k)
            if w is not None:
                deps.append(w)
            deps.extend(self.readers.get(k, ()))
        if chan is not None:
            p = self.chan_last.get(chan)
            if p is not None:
                deps.append(p)
            self.chan_last[chan] = op
        seen = set()
        for d in deps:
            if d is op or id(d) in seen:
                continue
            seen.add(id(d))
            if d.chan is None and d.eng == "pe" and eng == "pe" and chan is None:
                continue
            op.deps.append(d)
            d.signal = True
        for k in op.reads:
            self.readers.setdefault(k, []).append(op)
        for k in op.writes:
            self.last_writer[k] = op
            self.readers[k] = []
        self.ops.append(op)
        return op

    def emit(self, nc):
        cnt = {}
        chans = []
        for op in self.ops:
            if op.chan is not None:
                key = ("chan", op.chan)
                if key not in cnt:
                    chans.append(op.chan)
                cnt[key] = cnt.get(key, 0) + 16 * op.ndma
                op.ticket = cnt[key]
            elif op.signal:
                key = ("eng", op.eng)
                cnt[key] = cnt.get(key, 0) + 1
                op.ticket = cnt[key]
        per_eng = {e: [o for o in self.ops if o.eng == e] for e in ENGS}
        sems = {}
        with contextlib.ExitStack() as st:
            for e in ENGS:
                sems[("eng", e)] = st.enter_context(nc.semaphore("s_" + e))
            for c in chans:
                sems[("chan", c)] = st.enter_context(nc.semaphore("c_" + str(c)))
            block = st.enter_context(nc.Block())

            def body(ename):
                def run(eng):
                    waited = {}
                    for op in per_eng[ename]:
                        for d in op.deps:
                            skey = ("chan", d.chan) if d.chan is not None else ("eng", d.eng)
                            if waited.get(skey, 0) >= d.ticket:
                                continue
                            waited[skey] = d.ticket
                            eng.wait_ge(sems[skey], d.ticket)
                        ins = op.fn(eng)
                        if op.chan is not None:
                            if not isinstance(ins, (list, tuple)):
                                ins = [ins]
                            assert len(ins) == op.ndma, (len(ins), op.ndma)
                            for i_ in ins:
                                i_.then_inc(sems[("chan", op.chan)], 16)
                        elif op.signal:
                            ins.then_inc(sems[("eng", ename)], 1)
                    if ename == "sp":
                        for c in chans:
                            eng.wait_ge(sems[("chan", c)], cnt[("chan", c)])
                return run

            block.tensor(body("pe"))
            block.scalar(body("act"))
            block.vector(body("dve"))
            block.gpsimd(body("pool"))
            block.sync(body("sp"))
        return cnt


def build_nc(depth=4, ntiles=9, raw_out=False, maxops=None, verbose=False):
    nc = bass.Bass("TRN2", target_bir_lowering=False)
    S = Sched()

    def din(name, shape, dt=F32):
        return nc.dram_tensor(name, list(shape), dt, kind="ExternalInput").ap()

    xh = din("xh", [NTOK, D])
    cT = din("cT", [128, 8])
    vecs = din("vecs", [128, 80])
    adab = din("adab", [128, 112])
    ada_w = din("ada_w", [4, D, 3 * D])
    kv_ada_w = din("kv_ada_w", [D, 2 * D])
    a_w_in = din("a_w_in", [2, D, 4096])
    a_w_group = din("a_w_group", [2, 4, 512, 512])
    a_w_out = din("a_w_out", [2, 2048, D])
    w_kv = din("w_kv", [D, 256])
    b_w_in = din("b_w_in", [2, D, 2048])
    b_w_out = din("b_w_out", [2, D, D])
    sinks = din("sinks", [128, 32])
    ebias = din("ebias", [128, 16 * 256])
    valid = din("valid", [128, 1])
    corr = din("corr", [128, 64])
    fgb = din("fgb", [128, D])
    y = nc.dram_tensor("y", [TOK, D], F32, kind="ExternalOutput").ap()
    NSCR = 32
    wscr = nc.dram_tensor("wscr", [NSCR, 128, 8192], BF16, kind="Internal").ap()

    with contextlib.ExitStack() as st:
        def sb(name, shape, dt):
            return st.enter_context(nc.sbuf_tensor(name, list(shape), dt))

        def psb(name):
            return st.enter_context(nc.psum_tensor(name, [128, 512], F32))

        xs = sb("xs", [128, 8, TT], F32)
        NSTG = 4
        stg_in = sb("stg_in", [128, NSTG, D], F32)
        xsq = sb("xsq", [128, 8, TT], BF16)
        h = sb("h", [128, 8, TT], BF16)
        rs = sb("rs", [128, TT], F32)
        tmp = sb("tmp", [128, 2, TT], F32)
        u = sb("u", [128, 2, 4, 16 + TT], F32)
        sA = sb("sA", [128, 16 + TT], F32)
        sB = sb("sB", [128, 16 + TT], F32)
        yb = sb("yb", [128, 2, TT], BF16)
        bufA = sb("bufA", [128, 8, TT], BF16)
        bufB = sb("bufB", [128, 8, TT], BF16)
        m = sb("m", [128, 16, TT], BF16)
        uh = sb("uh", [128, 2, 16, 16], F32)
        kd = sb("kd", [128, 2, 9, 128], BF16)
        vx = sb("vx", [128, 9, 2, 66], BF16)
        E = sb("E", [128, 16, 256], BF16)
        pe_ = sb("pe", [128, 4, 512], BF16)
        pT = sb("pT", [128, 4, 512], BF16)
        otm = sb("otm", [128, 2, 256], BF16)
        den4 = sb("den4", [128, 2, 4], F32)
        identb = sb("identb", [128, 128], BF16)
        W = sb("W", [128, 3, 8192], BF16)
        WG = sb("WG", [128, 8192], BF16)
        ident = sb("ident", [128, 128], F32)
        ones = sb("ones", [128, 128], BF16)
        one1 = sb("one1", [1, 1], F32)
        cT_sb = sb("cT_sb", [128, 8], F32)
        cact = sb("cact", [128, 8], BF16)
        vecs_sb = sb("vecs_sb", [128, 80], F32)
        adab_sb = sb("adab_sb", [128, 112], F32)
        mods = sb("mods", [128, 112], F32)
        gs = sb("gs", [128, 40], F32)
        esink = sb("esink", [128, 32], F32)
        valid_sb = sb("valid_sb", [128, 1], F32)
        corr_sb = sb("corr_sb", [128, 64], F32)
        epsb = sb("epsb", [128, 1], F32)
        rst = sb("rst", [128, 4], F32)
        fgb_sb = sb("fgb_sb", [128, D], F32)
        pb = [psb("pb%d" % i) for i in range(8)]
        row_sb = tmp[0:1, :, :].rearrange("p a b -> p (a b)")
        ROWK = [("tmp", 0), ("tmp", 1)]
        print("sbuf bytes remaining:", nc.sbuf_bytes_remaining)

        V_NG, V_KVG, V_FG, V_AS = 0, 32, 40, 48

        tiles = [(0, HALO)] + [(HALO + i * TT, TT) for i in range(TOK // TT)]
        tiles = tiles[:ntiles]
        gblocks = [(ti_, blk_) for ti_, (t0_, T_) in enumerate(tiles) for blk_ in range(T_ // 128)]
        ldstate = {"n": 0}

        def issue_load():
            n = ldstate["n"]
            if n >= len(gblocks):
                return
            ldstate["n"] += 1
            ti_, blk_ = gblocks[n]
            s_ = n % NSTG
            r0 = tiles[ti_][0] + blk_ * 128
            S.add("sp", lambda e, s_=s_, r0=r0: e.dma_start(out=stg_in[:, s_, :], in_=xh[r0:r0 + 128, :]),
                  writes=[("stg_in", s_)], chan=("ldx", s_))

        for _ in range(NSTG):
            issue_load()
        S.add("sp", lambda e: [e.dma_start(out=cT_sb[:], in_=cT),
                               e.dma_start(out=vecs_sb[:], in_=vecs),
                               e.dma_start(out=adab_sb[:], in_=adab),
                               e.dma_start(out=esink[:], in_=sinks),
                               e.dma_start(out=valid_sb[:], in_=valid),
                               e.dma_start(out=corr_sb[:], in_=corr),
                               e.dma_start(out=fgb_sb[:], in_=fgb)],
              writes=["cT_sb", "vecs", "adab", "esink", "valid", "corr", "fgb"], chan="init", ndma=7)
        S.add("pool", lambda e: e.memset(ident[:], 0.0), writes=["ident"])
        S.add("pool", lambda e: e.affine_select(out=ident[:], in_=ident[:], pattern=[[-1, 128]],
                                                 compare_op=ALU.not_equal, fill=1.0, base=0,
                                                 channel_multiplier=1),
              reads=["ident"], writes=["ident"])
        S.add("pool", lambda e: e.memset(ones[:], 1.0), writes=["ones"])
        S.add("dve", lambda e: e.tensor_copy(out=identb[:], in_=ident[:]), reads=["ident"], writes=["identb"])
        S.add("pool", lambda e: e.memset(one1[:], 1.0), writes=["one1"])
        S.add("pool", lambda e: e.memset(epsb[:], EPS), writes=["epsb"])
        S.add("pool", lambda e: e.memset(uh[:].rearrange("p a b c -> p (a b c)"), 0.0), writes=[("uh", l, c) for l in range(2) for c in range(16)])
        S.add("pool", lambda e: e.memset(vx[:].rearrange("p a b c -> p (a b c)"), 1.0), writes=[("vx", r) for r in range(9)])
        S.add("act", lambda e: e.activation(out=cact[:], in_=cT_sb[:], func=AF.Silu),
              reads=["cT_sb"], writes=["cact"])
        for hf in range(2):
            stg = u[:, hf, :, :].rearrange("p a b -> p (a b)")[:, 0:2048]
            keys = [("u", hf, j) for j in range(4)]
            S.add("sp", lambda e, stg=stg, hf=hf: e.dma_start(out=stg, in_=ebias[:, hf * 2048:(hf + 1) * 2048]),
                  writes=keys, chan="ebias%d" % hf)
            S.add("act", lambda e, stg=stg, hf=hf: e.activation(
                out=E[:, hf * 8:(hf + 1) * 8, :].rearrange("p a b -> p (a b)"), in_=stg, func=AF.Exp),
                reads=keys, writes=["E"])
        S.add("act", lambda e: e.activation(out=esink[:], in_=esink[:], func=AF.Exp),
              reads=["esink"], writes=["esink"])

        wstate = {"scr": {}, "nscr": 0}

        def piece_srcs(name):
            kind = name[0]
            if kind == "ada":
                _, l_, cb = name
                return [(lambda s: v3(s, 8, 1024), kpn(ada_w[l_][:, cb * 1024:(cb + 1) * 1024]))], False
            if kind == "kvada":
                cb = name[1]
                return [(lambda s: v3(s, 8, 1024), kpn(kv_ada_w[:, cb * 1024:(cb + 1) * 1024]))], False
            if kind == "UZ":
                _, l_, g = name
                src = a_w_in[l_].rearrange("(k p) (z g n) -> p k z g n", p=128, z=2, g=4, n=512)
                return [(lambda s: v3(s, 8, 1024)[:, :, 0:512], src[:, :, 0, g, :]),
                        (lambda s: v3(s, 8, 1024)[:, :, 512:1024], src[:, :, 1, g, :])], True
            if kind == "WO":
                _, l_, half = name
                return [(lambda s: v3(s, 8, 1024), kpn(a_w_out[l_][half * 1024:(half + 1) * 1024, :]))], True
            if kind == "KV":
                f = lambda a, b: (lambda s: v3(s[:, 0:8 * 384], 8, 384)[:, :, a:b])
                return [(f(0, 64), kpn(w_kv[:, 0:64])), (f(64, 128), kpn(w_kv[:, 0:64])),
                        (f(128, 192), kpn(w_kv[:, 64:128])), (f(192, 256), kpn(w_kv[:, 64:128])),
                        (f(256, 384), kpn(w_kv[:, 128:256]))], True
            if kind == "BQZ":
                _, jl_, half = name
                return [(lambda s: v3(s, 8, 1024), kpn(b_w_in[jl_][:, half * 1024:(half + 1) * 1024]))], True
            if kind == "BO":
                return [(lambda s: v3(s, 8, 1024), kpn(b_w_out[name[1]]))], True
            if kind == "WG":
                l_ = name[1]
                return [(lambda s: s.rearrange("p (g i n) -> p g i n", g=4, i=4, n=512),
                         a_w_group[l_].rearrange("g (i p) n -> p g i n", p=128))], True
            raise KeyError(name)

        def wb_tile(name):
            kind = name[0]
            if kind in ("UZ", "WO", "WG"):
                return 1 + name[1]
            return 3

        def emit_load(name, slot_ap, key, ti_req=0):
            srcs, keep = piece_srcs(name)
            if keep and name in wstate["scr"]:
                si = wstate["scr"][name]
                S.add("pool", lambda e: e.dma_start(out=slot_ap, in_=wscr[si]),
                      reads=[("scr", si)], writes=[key], chan=("w", key))
            else:
                S.add("pool", lambda e: [e.dma_start(out=f(slot_ap), in_=s_) for f, s_ in srcs],
                      writes=[key], chan=("w", key), ndma=len(srcs))
                if keep and ti_req >= min(wb_tile(name), ntl - 1):
                    si = wstate["nscr"]
                    wstate["nscr"] += 1
                    assert si < NSCR
                    wstate["scr"][name] = si
                    S.add("sp", lambda e: e.dma_start(out=wscr[si], in_=slot_ap),
                          reads=[key], writes=[("scr", si)], chan=("wb", si % 4))

        ntl = min(ntiles, 1 + TOK // TT)
        reqs = []
        wg_uses = []
        req_ti = []
        wg_ti = []
        def mod_names(lm):
            if lm == 4:
                return [("kvada", 0), ("kvada", 1)]
            return [("ada", lm, cb) for cb in range(3)]

        hooks = {}
        if depth >= 2:
            hooks[(0, 0)] = 1
        if depth >= 3:
            hooks[(0, 1)] = 4
            hooks[(1, 0)] = 2
        if depth >= 4:
            hooks[(1, 1)] = 3
        hooks = {k: v for k, v in hooks.items() if k[0] < ntl and k[1] < min(depth, 2)}
        for ti_ in range(ntl):
            n0_, w0_ = len(reqs), len(wg_uses)
            for l_ in range(min(depth, 2)):
                if ti_ == 0 and l_ == 0:
                    reqs += mod_names(0)
                hk_ = mod_names(hooks[(ti_, l_)]) if (ti_, l_) in hooks else []
                hk_ = hk_ + [None] * (3 - len(hk_))
                reqs += [("UZ", l_, 3), ("UZ", l_, 2)]
                for g_, hn_ in zip((1, 0), hk_[:2]):
                    if hn_ is not None:
                        reqs.append(hn_)
                    reqs.append(("UZ", l_, g_))
                if hk_[2] is not None:
                    reqs.append(hk_[2])
                reqs += [("WO", l_, 0), ("WO", l_, 1)]
                wg_uses.append(("WG", l_))
            if depth >= 3:
                reqs.append(("KV",))
                if ti_ >= 1:
                    for jl_ in range(depth - 2):
                        reqs += [("BQZ", jl_, 0), ("BQZ", jl_, 1), ("BO", jl_)]
            req_ti += [ti_] * (len(reqs) - n0_)
            wg_ti += [ti_] * (len(wg_uses) - w0_)
        wq = {"issued": 0, "got": 0, "wg": 0}

        def wq_issue():
            i = wq["issued"]
            if i >= len(reqs):
                return
            wq["issued"] += 1
            sidx = i % 3
            emit_load(reqs[i], W[:, sidx, :], ("W", sidx), req_ti[i])

        def wq_get(name):
            i = wq["got"]
            assert reqs[i] == name, (reqs[i], name)
            assert i < wq["issued"], (i, name)
            wq["got"] += 1
            sidx = i % 3
            return W[:, sidx, :], ("W", sidx)

        def wq_done():
            wq_issue()

        def wg_issue():
            i = wq["wg"]
            if i >= len(wg_uses):
                return
            wq["wg"] += 1
            emit_load(wg_uses[i], WG[:], "WG", wg_ti[i])

        def v3(ap2d, k, n):
            return ap2d.rearrange("p (k n) -> p k n", k=k, n=n)

        def kpn(src2d):
            return src2d.rearrange("(k p) n -> p k n", p=128)

        bank_rr = {"i": 0}

        def mm_bank(lo=1, hi=7):
            n = hi - lo + 1
            b = lo + (bank_rr["i"] % n)
            bank_rr["i"] += 1
            return b

        def mod_piece(name, col0):
            sl, key = wq_get(name)
            w3 = v3(sl, 8, 1024)
            for n in range(2):
                b = mm_bank()
                for k in range(8):
                    S.add("pe", lambda e, b=b, k=k, n=n: e.matmul(
                        pb[b][0:1, :], lhsT=cact[:, k:k + 1], rhs=w3[:, k, n * 512:(n + 1) * 512],
                        start=(k == 0), stop=(k == 7)),
                        reads=[key, "cact"], writes=[("pb", b)])
                S.add("dve", lambda e, b=b, n=n: e.tensor_copy(out=row_sb[0:1, n * 512:(n + 1) * 512], in_=pb[b][0:1, :]),
                      reads=[("pb", b)], writes=ROWK)
            wq_done()
            b = mm_bank()
            for j in range(8):
                S.add("pe", lambda e, b=b, j=j: e.matmul(
                    pb[b][:, j:j + 1], lhsT=row_sb[0:1, j * 128:(j + 1) * 128], rhs=one1[0:1, 0:1],
                    start=True, stop=True),
                    reads=ROWK + ["one1"], writes=[("pb", b)])
            S.add("dve", lambda e, b=b: e.tensor_tensor(out=mods[:, col0:col0 + 8], in0=pb[b][:, 0:8],
                                                        in1=adab_sb[:, col0:col0 + 8], op=ALU.add),
                  reads=[("pb", b), "adab"], writes=["mods"])

        def mod_one(name):
            if name[0] == "ada":
                mod_piece(name, name[1] * 24 + name[2] * 8)
            else:
                mod_piece(name, 96 + name[1] * 8)

        def mods_finish(l):
            sc = (l * 24 + 8) if l < 4 else 104
            ng = (V_NG + l * 8) if l < 4 else V_KVG
            S.add("dve", lambda e: e.scalar_tensor_tensor(
                out=gs[:, l * 8:(l + 1) * 8], in0=mods[:, sc:sc + 8], scalar=1.0, in1=vecs_sb[:, ng:ng + 8],
                op0=ALU.add, op1=ALU.mult),
                reads=["mods", "vecs"], writes=["gs"])

        def mods_layer(l):
            for nm in mod_names(l):
                mod_one(nm)
            mods_finish(l)

        S.mark("mods")
        for _ in range(3):
            wq_issue()
        wg_issue()

        def shift_col(l):
            return (l * 24) if l < 4 else 96

        def gate_col(l):
            return l * 24 + 16

        def front_stats(T):
            b = 0
            for c in range(8):
                s_ = c
                S.add("act", lambda e, c=c, s_=s_: e.activation(out=xsq[:, s_, :T], in_=xs[:, c, :T], func=AF.Square),
                      reads=[("xs", c)], writes=[("xsq", s_)])
                S.add("pe", lambda e, c=c, s_=s_: e.matmul(pb[b][:, :T], lhsT=ones[:], rhs=xsq[:, s_, :T],
                                                            start=(c == 0), stop=(c == 7)),
                      reads=[("xsq", s_), "ones"], writes=[("pb", b)])
            S.add("act", lambda e: e.activation(out=rs[:, :T], in_=pb[b][:, :T], func=AF.Ln,
                                                bias=epsb[:], scale=1.0 / D),
                  reads=[("pb", b), "epsb"], writes=["rs"])
            S.add("act", lambda e: e.activation(out=rs[:, :T], in_=rs[:, :T], func=AF.Exp, scale=-0.5),
                  reads=["rs"], writes=["rs"])

        def keep_warm(n=8, extra=()):
            for _ in range(n):
                S.add("pe", lambda e: e.matmul(pb[0][:, :].rearrange("p (a b) -> p a b", a=2), lhsT=ones[:],
                                               rhs=E[:, 0:2, :], start=True, stop=True),
                      reads=["E", "ones"] + list(extra), writes=[("pb", 0)])

        def front_h(T, gcol, scol, warm=True):
            if warm:
                keep_warm(6)
                keep_warm(10, extra=["rs"])
            for c in range(8):
                s_ = c % 2
                S.add("dve", lambda e, c=c, s_=s_: e.scalar_tensor_tensor(
                    out=tmp[:, s_, :T], in0=xs[:, c, :T], scalar=gs[:, gcol + c:gcol + c + 1], in1=rs[:, :T],
                    op0=ALU.mult, op1=ALU.mult),
                    reads=[("xs", c), "rs", "gs"], writes=[("tmp", s_)])
                S.add("act", lambda e, c=c, s_=s_: e.activation(
                    out=h[:, c, :T], in_=tmp[:, s_, :T], func=AF.Identity, bias=mods[:, scol + c:scol + c + 1]),
                    reads=[("tmp", s_), "mods"], writes=[("h", c)])

        hkeys = [("h", c) for c in range(8)]

        def layer_a(l, ti, T):
            front_stats(T)
            front_h(T, l * 8, shift_col(l))
            wg_key = "WG"
            wg4 = WG[:].rearrange("p (g i n) -> p g i n", g=4, i=4, n=512)

            def uz(g):
                sl, key = wq_get(("UZ", l, g))
                w3 = v3(sl, 8, 1024)
                us = g % 2
                for j in range(4):
                    b = mm_bank()
                    for k in range(8):
                        S.add("pe", lambda e, b=b, k=k, j=j: e.matmul(
                            pb[b][:, :T], lhsT=w3[:, k, j * 128:(j + 1) * 128], rhs=h[:, k, :T],
                            start=(k == 0), stop=(k == 7)),
                            reads=[key, ("h", k)], writes=[("pb", b)])
                    ukey = ("u", us, j)
                    if ti == 0:
                        S.add("act", lambda e, b=b, j=j: e.activation(
                            out=u[:, us, j, 16:16 + T], in_=pb[b][:, :T], func=AF.Identity, scale=valid_sb[:, 0:1]),
                            reads=[("pb", b), "valid"], writes=[ukey])
                    else:
                        S.add("act", lambda e, b=b, j=j: e.activation(
                            out=u[:, us, j, 16:16 + T], in_=pb[b][:, :T], func=AF.Copy),
                            reads=[("pb", b)], writes=[ukey])
                    S.add("act", lambda e, j=j: e.activation(out=u[:, us, j, 0:16], in_=uh[:, l, g * 4 + j, :], func=AF.Copy),
                          reads=[("uh", l, g * 4 + j)], writes=[ukey])
                    if ti == 0:
                        S.add("act", lambda e, b=b, j=j: e.activation(
                            out=uh[:, l, g * 4 + j, :], in_=pb[b][:, T - 16:T], func=AF.Identity, scale=valid_sb[:, 0:1]),
                            reads=[("pb", b), "valid"], writes=[("uh", l, g * 4 + j)])
                    else:
                        S.add("act", lambda e, b=b, j=j: e.activation(
                            out=uh[:, l, g * 4 + j, :], in_=pb[b][:, T - 16:T], func=AF.Copy),
                            reads=[("pb", b)], writes=[("uh", l, g * 4 + j)])
                for j in range(4):
                    b = mm_bank()
                    for k in range(8):
                        S.add("pe", lambda e, b=b, k=k, j=j: e.matmul(
                            pb[b][:, :T], lhsT=w3[:, k, 512 + j * 128:512 + (j + 1) * 128], rhs=h[:, k, :T],
                            start=(k == 0), stop=(k == 7)),
                            reads=[key, ("h", k)], writes=[("pb", b)])
                    S.add("act", lambda e, b=b, j=j: e.activation(
                        out=bufB[:, us * 4 + j, :T], in_=pb[b][:, :T], func=AF.Silu),
                        reads=[("pb", b)], writes=[("bufB", us * 4 + j)])
                wq_done()

            def pool_g(g):
                us = g % 2
                w = 2 ** (g + 1)
                for j in range(4):
                    ukey = ("u", us, j)
                    src, skey = u[:, us, j, :], ukey
                    eng_ = "dve"
                    bufs = [(sA, "sA"), (sB, "sB")]
                    lo = 0
                    for k in range(g + 1):
                        d = 2 ** k
                        dst, dkey = bufs[k % 2]
                        S.add(eng_, lambda e, src=src, dst=dst, lo=lo, d=d: e.tensor_tensor(
                            out=dst[:, lo + d:16 + T], in0=src[:, lo + d:16 + T], in1=src[:, lo:16 + T - d], op=ALU.add),
                            reads=[skey], writes=[dkey])
                        src, skey = dst, dkey
                        lo += d
                    if ti == 1:
                        S.add(eng_, lambda e, src=src: e.tensor_tensor(
                            out=src[:, 16:32], in0=src[:, 16:32], in1=corr_sb[:, g * 16:(g + 1) * 16], op=ALU.mult),
                            reads=[skey, "corr"], writes=[skey])
                    S.add("dve", lambda e, src=src, j=j: e.scalar_tensor_tensor(
                        out=bufA[:, us * 4 + j, :T], in0=src[:, 16:16 + T], scalar=1.0 / w, in1=u[:, us, j, 16:16 + T],
                        op0=ALU.mult, op1=ALU.subtract),
                        reads=[skey, ukey], writes=[("bufA", us * 4 + j)])

            def ymat(g):
                us = g % 2
                for j in range(4):
                    b = mm_bank()
                    for i in range(4):
                        S.add("pe", lambda e, b=b, i=i, j=j: e.matmul(
                            pb[b][:, :T], lhsT=wg4[:, g, i, j * 128:(j + 1) * 128], rhs=bufA[:, us * 4 + i, :T],
                            start=(i == 0), stop=(i == 3)),
                            reads=[wg_key, ("bufA", us * 4 + i)], writes=[("pb", b)])
                    kk = g * 4 + j
                    ys = kk % 2
                    S.add("act", lambda e, b=b, kk=kk, ys=ys: e.activation(
                        out=yb[:, ys, :T], in_=pb[b][:, :T], func=AF.Identity,
                        scale=vecs_sb[:, V_AS + l * 16 + kk:V_AS + l * 16 + kk + 1]),
                        reads=[("pb", b), "vecs"], writes=[("yb", ys)])
                    S.add("dve", lambda e, j=j, kk=kk, ys=ys: e.tensor_tensor(
                        out=m[:, kk, :T], in0=yb[:, ys, :T], in1=bufB[:, us * 4 + j, :T], op=ALU.mult),
                        reads=[("yb", ys), ("bufB", us * 4 + j)], writes=[("m", kk)])

            hk = list(mod_names(hooks[(ti, l)])) if (ti, l) in hooks else []
            hk = hk + [None] * (3 - len(hk))
            uz(3)
            pool_g(3)
            uz(2)
            pool_g(2)
            ymat(3)
            if hk[0] is not None:
                mod_one(hk[0])
            uz(1)
            pool_g(1)
            ymat(2)
            if hk[1] is not None:
                mod_one(hk[1])
            uz(0)
            pool_g(0)
            ymat(1)
            ymat(0)
            if hk[2] is not None:
                mod_one(hk[2])
            if (ti, l) in hooks:
                mods_finish(hooks[(ti, l)])
            wg_issue()
            wo = []
            for half in range(2):
                sl, key = wq_get(("WO", l, half))
                wo.append((v3(sl, 8, 1024), key))
            gc = gate_col(l)
            for c in range(8):
                b = mm_bank()
                for k in range(16):
                    w3, key = wo[k // 8]
                    S.add("pe", lambda e, b=b, k=k, c=c, w3=w3: e.matmul(
                        pb[b][:, :T], lhsT=w3[:, k % 8, c * 128:(c + 1) * 128], rhs=m[:, k, :T],
                        start=(k == 0), stop=(k == 15)),
                        reads=[key, ("m", k)], writes=[("pb", b)])
                S.add("dve", lambda e, b=b, c=c: e.scalar_tensor_tensor(
                    out=xs[:, c, :T], in0=pb[b][:, :T], scalar=mods[:, gc + c:gc + c + 1], in1=xs[:, c, :T],
                    op0=ALU.mult, op1=ALU.add),
                    reads=[("pb", b), "mods", ("xs", c)], writes=[("xs", c)])
            wq_done()
            wq_done()

        def ring_of(gb):
            return 8 if gb < 0 else gb % 8

        def kv_part(ti, T):
            front_h(T, 32, shift_col(4))
            sl, key = wq_get(("KV",))
            w3 = v3(sl[:, 0:8 * 384], 8, 384)
            if ti == 0:
                blocks = [(-1, 1)]
            else:
                blocks = [(4 * (ti - 1) + i, i) for i in range(T // 128)]
            lb0 = blocks[0][1]
            nb = len(blocks)
            r0 = ring_of(blocks[0][0])
            rkeys = [("kd", ring_of(gb)) for gb, _ in blocks]
            vkeys = [("vx", ring_of(gb)) for gb, _ in blocks]
            for g in range(2):
                b = mm_bank()
                for k in range(8):
                    S.add("pe", lambda e, b=b, k=k, g=g: e.matmul(
                        pb[b][:, :nb * 128], lhsT=w3[:, k, g * 128:(g + 1) * 128],
                        rhs=h[:, k, lb0 * 128:(lb0 + nb) * 128], start=(k == 0), stop=(k == 7)),
                        reads=[key, ("h", k)], writes=[("pb", b)])
                S.add("dve", lambda e, b=b, g=g: e.tensor_copy(
                    out=kd[:, g, r0:r0 + nb, :], in_=pb[b][:, :nb * 128].rearrange("p (a b) -> p a b", a=nb)),
                    reads=[("pb", b)], writes=rkeys)
            b = mm_bank()
            for i, (gb, lb) in enumerate(blocks):
                for k in range(8):
                    S.add("pe", lambda e, b=b, k=k, i=i, lb=lb: e.matmul(
                        pb[b][:, i * 128:(i + 1) * 128], lhsT=h[:, k, lb * 128:(lb + 1) * 128],
                        rhs=w3[:, k, 256:384], start=(k == 0), stop=(k == 7)),
                        reads=[key, ("h", k)], writes=[("pb", b)])
            wq_done()
            S.add("dve", lambda e, b=b: e.tensor_copy(
                out=vx[:, r0:r0 + nb, :, 0:64],
                in_=pb[b][:, :nb * 128].rearrange("p (a g d) -> p a g d", a=nb, g=2)),
                reads=[("pb", b)], writes=vkeys)
            if ti == 0:
                S.add("dve", lambda e: e.tensor_scalar(
                    out=vx[:, 8, :, :].rearrange("p a b -> p (a b)"),
                    in0=vx[:, 8, :, :].rearrange("p a b -> p (a b)"),
                    scalar1=valid_sb[:, 0:1], scalar2=None, op0=ALU.mult),
                    reads=[("vx", 8), "valid"], writes=[("vx", 8)])

        def layer_b(l, ti, T):
            jl = l - 2
            front_h(T, l * 8, shift_col(l))
            qz = []
            for half in range(2):
                sl, key = wq_get(("BQZ", jl, half))
                qz.append((v3(sl, 8, 1024), key))
            for half in range(2):
                w3, key = qz[half]
                for c in range(8):
                    b = mm_bank()
                    for k in range(8):
                        S.add("pe", lambda e, b=b, k=k, c=c, w3=w3: e.matmul(
                            pb[b][:, :T], lhsT=w3[:, k, c * 128:(c + 1) * 128], rhs=h[:, k, :T],
                            start=(k == 0), stop=(k == 7)),
                            reads=[key, ("h", k)], writes=[("pb", b)])
                    if half == 0:
                        S.add("act", lambda e, b=b, c=c: e.activation(out=bufA[:, c, :T], in_=pb[b][:, :T], func=AF.Copy),
                              reads=[("pb", b)], writes=[("bufA", c)])
                    else:
                        S.add("act", lambda e, b=b, c=c: e.activation(out=bufB[:, c, :T], in_=pb[b][:, :T], func=AF.Silu),
                              reads=[("pb", b)], writes=[("bufB", c)])
                wq_done()
            units = [(blk, q) for blk in range(T // 128) for q in range(4)]
            sbank = {"i": 0}

            def scores(ui):
                blk, q = units[ui]
                gblk = 4 * (ti - 1) + blk
                g = q // 2
                banks = []
                for par in range(2):
                    banks.append(1 + (sbank["i"] % 4))
                    sbank["i"] += 1
                for hh in range(2):
                    c = 2 * q + hh
                    for kb in range(2):
                        rg = ring_of(gblk - 1 + kb)
                        for par in range(2):
                            b = banks[par]
                            r0 = par * 64
                            S.add("pe", lambda e, b=b, hh=hh, kb=kb, rg=rg, c=c, r0=r0: e.matmul(
                                pb[b][:, (hh * 2 + kb) * 128:(hh * 2 + kb + 1) * 128],
                                lhsT=kd[r0:r0 + 64, g, rg, :], rhs=bufA[r0:r0 + 64, c, blk * 128:(blk + 1) * 128],
                                start=True, stop=True),
                                reads=[("kd", rg), ("bufA", c)], writes=[("pb", b)])
                return banks

            def stage_x(ui, banks):
                blk, q = units[ui]
                for par in range(2):
                    b = banks[par]
                    ps_ = (ui * 2 + par) % 4
                    S.add("act", lambda e, b=b, ps_=ps_: e.activation(out=pe_[:, ps_, :], in_=pb[b][:, :], func=AF.Exp,
                                                                      scale=0.125),
                          reads=[("pb", b)], writes=[("pe", ps_)])
                    h0 = 4 * q + par
                    S.add("dve", lambda e, ps_=ps_, h0=h0: e.tensor_tensor(
                        out=pT[:, ps_, :].rearrange("p (a b) -> p a b", a=2),
                        in0=pe_[:, ps_, :].rearrange("p (a b) -> p a b", a=2),
                        in1=E[:, h0:h0 + 3:2, :], op=ALU.mult),
                        reads=[("pe", ps_), "E"], writes=[("pT", ps_)])

            def stage_v(ui):
                blk, q = units[ui]
                gblk = 4 * (ti - 1) + blk
                g = q // 2
                ob = 5 + (ui % 2)
                sl_ = ui % 2
                for par in range(2):
                    ps_ = (ui * 2 + par) % 4
                    p4 = pT[:, ps_, :].rearrange("p (a k b) -> p a k b", a=2, k=2)
                    for hh in range(2):
                        hl = 2 * hh + par
                        for kb in range(2):
                            rg = ring_of(gblk - 1 + kb)
                            S.add("pe", lambda e, kb=kb, rg=rg, hh=hh, hl=hl, p4=p4: e.matmul(
                                pb[ob][:, hl * 128:hl * 128 + 65], lhsT=p4[:, hh, kb, :], rhs=vx[:, rg, g, 0:65],
                                start=(kb == 0), stop=(kb == 1)),
                                reads=[("vx", rg), ("pT", ps_)], writes=[("pb", ob)])
                ec = jl * 16 + 4 * q
                o4 = pb[ob][:, :].rearrange("p (a b) -> p a b", a=4)
                S.add("dve", lambda e: e.tensor_tensor(
                    out=den4[:, sl_, :], in0=o4[:, :, 64], in1=esink[:, ec:ec + 4], op=ALU.add),
                    reads=[("pb", ob), "esink"], writes=[("den4", sl_)])
                S.add("dve", lambda e: e.reciprocal(out=den4[:, sl_, :], in_=den4[:, sl_, :]),
                      reads=[("den4", sl_)], writes=[("den4", sl_)])
                S.add("dve", lambda e: e.tensor_tensor(
                    out=otm[:, sl_, :].rearrange("p (a b) -> p a b", a=4), in0=o4[:, :, 0:64],
                    in1=den4[:, sl_, :].unsqueeze(2).to_broadcast([128, 4, 64]), op=ALU.mult),
                    reads=[("pb", ob), ("den4", sl_)], writes=[("otm", sl_)])

            def stage_t(ui):
                blk, q = units[ui]
                sl_ = ui % 2
                tb = 7 if ui % 2 == 0 else 0
                tbv = pb[tb][:, :].bitcast(BF16)
                for c2 in range(2):
                    S.add("pe", lambda e, c2=c2: e.transpose(
                        out=tbv[:, c2 * 128:(c2 + 1) * 128], in_=otm[:, sl_, c2 * 128:(c2 + 1) * 128],
                        identity=identb[:]),
                        reads=[("otm", sl_), "identb"], writes=[("pb", tb)])
                S.add("dve", lambda e: e.tensor_tensor(
                    out=m[:, 2 * q:2 * q + 2, blk * 128:(blk + 1) * 128],
                    in0=tbv[:, 0:256].rearrange("p (a b) -> p a b", a=2),
                    in1=bufB[:, 2 * q:2 * q + 2, blk * 128:(blk + 1) * 128], op=ALU.mult),
                    reads=[("pb", tb), ("bufB", 2 * q), ("bufB", 2 * q + 1)],
                    writes=[("m", 2 * q), ("m", 2 * q + 1)])

            sl, key = wq_get(("BO", jl))
            nu = len(units)
            sb_ = {0: scores(0)}
            if nu > 1:
                sb_[1] = scores(1)
            stage_x(0, sb_[0])
            for ui in range(nu):
                if ui + 2 < nu:
                    sb_[ui + 2] = scores(ui + 2)
                if ui + 1 < nu:
                    stage_x(ui + 1, sb_[ui + 1])
                stage_v(ui)
                if ui >= 1:
                    stage_t(ui - 1)
            stage_t(nu - 1)
            w3 = v3(sl, 8, 1024)
            gc = gate_col(l)
            for c in range(8):
                b = mm_bank()
                for k in range(8):
                    S.add("pe", lambda e, b=b, k=k, c=c: e.matmul(
                        pb[b][:, :T], lhsT=w3[:, k, c * 128:(c + 1) * 128], rhs=m[:, k, :T],
                        start=(k == 0), stop=(k == 7)),
                        reads=[key, ("m", k)], writes=[("pb", b)])
                S.add("dve", lambda e, b=b, c=c: e.scalar_tensor_tensor(
                    out=xs[:, c, :T], in0=pb[b][:, :T], scalar=mods[:, gc + c:gc + c + 1], in1=xs[:, c, :T],
                    op0=ALU.mult, op1=ALU.add),
                    reads=[("pb", b), "mods", ("xs", c)], writes=[("xs", c)])
            wq_done()

        stcount = {"i": 0}
        S.mark("tiles")
        gbi = 0
        for ti, (t0, T) in enumerate(tiles):
            S.mark("tile%d" % ti)
            for blk in range(T // 128):
                s_ = gbi % NSTG
                gbi += 1
                for hc in range(2):
                    b = mm_bank()
                    for cc in range(4):
                        c = hc * 4 + cc
                        S.add("pe", lambda e, b=b, cc=cc, c=c, s_=s_: e.transpose(
                            out=pb[b][:, cc * 128:(cc + 1) * 128], in_=stg_in[:, s_, c * 128:(c + 1) * 128],
                            identity=ident[:]),
                            reads=[("stg_in", s_), "ident"], writes=[("pb", b)])
                    S.add("act", lambda e, b=b, hc=hc, blk=blk: e.activation(
                        out=xs[:, hc * 4:(hc + 1) * 4, blk * 128:(blk + 1) * 128],
                        in_=pb[b][:, :].rearrange("p (a b) -> p a b", a=4), func=AF.Copy),
                        reads=[("pb", b)], writes=[("xs", hc * 4 + cc) for cc in range(4)])
                issue_load()
            S.mark("tile%d-layers" % ti)
            nl = depth
            if nl >= 1:
                if ti == 0:
                    mods_layer(0)
                layer_a(0, ti, T)
            if nl >= 2:
                layer_a(1, ti, T)
            if nl >= 3:
                front_stats(T)
                kv_part(ti, T)
                if ti >= 1:
                    layer_b(2, ti, T)
                    if nl >= 4:
                        front_stats(T)
                        layer_b(3, ti, T)
            if ti == 0:
                continue
            nblk = T // 128
            if not raw_out:
                for c in range(8):
                    S.add("act", lambda e, c=c: e.activation(out=xsq[:, c, :T], in_=xs[:, c, :T], func=AF.Square),
                          reads=[("xs", c)], writes=[("xsq", c)])
                for blk in range(nblk):
                    for c in range(8):
                        S.add("pe", lambda e, c=c, blk=blk: e.matmul(
                            pb[0][:, blk:blk + 1], lhsT=xsq[:, c, blk * 128:(blk + 1) * 128], rhs=ones[:, 0:1],
                            start=(c == 0), stop=(c == 7)),
                            reads=[("xsq", c), "ones"], writes=[("pb", 0)])
                S.add("act", lambda e: e.activation(out=rst[:, 0:nblk], in_=pb[0][:, 0:nblk], func=AF.Ln,
                                                    bias=epsb[:], scale=1.0 / D),
                      reads=[("pb", 0), "epsb"], writes=["rst"])
                S.add("act", lambda e: e.activation(out=rst[:, 0:nblk], in_=rst[:, 0:nblk], func=AF.Exp, scale=-0.5),
                      reads=["rst"], writes=["rst"])
            for blk in range(nblk):
                so_ap = m[:, 4 * blk:4 * blk + 4, :].rearrange("p a b -> p (a b)").bitcast(F32)
                so_keys = [("m", 4 * blk + i_) for i_ in range(4)]
                for hc in range(2):
                    b = mm_bank(1, 7)
                    for cc in range(4):
                        c = hc * 4 + cc
                        S.add("pe", lambda e, b=b, cc=cc, c=c, blk=blk: e.transpose(
                            out=pb[b][:, cc * 128:(cc + 1) * 128], in_=xs[:, c, blk * 128:(blk + 1) * 128],
                            identity=ident[:]),
                            reads=[("xs", c), "ident"], writes=[("pb", b)])
                    if raw_out:
                        S.add("dve", lambda e, b=b, hc=hc, so_ap=so_ap: e.tensor_copy(
                            out=so_ap[:, hc * 512:(hc + 1) * 512], in_=pb[b][:, :]),
                            reads=[("pb", b)], writes=so_keys)
                    else:
                        S.add("dve", lambda e, b=b, hc=hc, so_ap=so_ap, blk=blk: e.scalar_tensor_tensor(
                            out=so_ap[:, hc * 512:(hc + 1) * 512], in0=pb[b][:, :],
                            scalar=rst[:, blk:blk + 1], in1=fgb_sb[:, hc * 512:(hc + 1) * 512],
                            op0=ALU.mult, op1=ALU.mult),
                            reads=[("pb", b), "rst", "fgb"], writes=so_keys)
                r0 = t0 - HALO + blk * 128
                S.add("sp", lambda e, so_ap=so_ap, r0=r0: e.dma_start(out=y[r0:r0 + 128, :], in_=so_ap),
                      reads=so_keys, chan=("sty", blk % 2))
        if verbose:
            for name, n in S.marks:
                print("mark", name, n)
            print("total ops", len(S.ops))
        if maxops is not None:
            S.ops = S.ops[:maxops]
        S.emit(nc)
    return nc


def _t5_buckets():
    i = np.arange(128)[:, None]
    j = np.arange(256)[None, :]
    n = np.maximum(i + 128 - j, 0)
    max_exact = 16
    large = max_exact + (np.log(np.maximum(n, 1) / max_exact) / math.log(128 / max_exact)
                         * (32 - max_exact)).astype(np.int32)
    large = np.minimum(large, 31)
    return np.where(n < max_exact, n, large).astype(np.int32)


def fm(v):
    v = np.asarray(v, np.float32).reshape(-1, 128)
    return np.ascontiguousarray(v.T)


def prep_inputs(x, c, norm_g, ada_w, ada_b, a_w_in, a_w_group, a_scale, a_w_out,
                kv_norm_g, kv_ada_w, kv_ada_b, w_kv, b_w_in, b_sinks, b_w_out,
                rel_bias, final_g):
    f32 = lambda a: np.ascontiguousarray(np.asarray(a, np.float32))
    x = f32(x)
    bk = _t5_buckets()
    rb = f32(rel_bias)
    pos = rb[bk]
    i = np.arange(128)[:, None]
    jj = np.arange(256)[None, :]
    rel = i + 128 - jj
    band = (rel >= 0) & (rel < 128)
    tab = np.where(band[:, :, None], pos, np.float32(NEG)).astype(np.float32)
    tab = tab.reshape(128, 2, 128, 16).transpose(2, 3, 1, 0)
    ebias = np.ascontiguousarray(tab.reshape(128, 16 * 256))
    vecs = np.concatenate([fm(norm_g[l]) for l in range(4)] + [fm(kv_norm_g), fm(final_g)]
                          + [fm(a_scale[l]) for l in range(2)], axis=1)
    adab = np.concatenate([fm(ada_b[l]) for l in range(4)] + [fm(kv_ada_b)], axis=1)
    sinks = np.ascontiguousarray(np.broadcast_to(f32(b_sinks).reshape(1, 32), (128, 32)))
    shared = {
        "vecs": f32(vecs), "adab": f32(adab), "ada_w": f32(ada_w), "kv_ada_w": f32(kv_ada_w),
        "a_w_in": f32(a_w_in), "a_w_group": f32(a_w_group), "a_w_out": f32(a_w_out),
        "w_kv": f32(w_kv), "b_w_in": f32(b_w_in), "b_w_out": f32(b_w_out),
        "sinks": sinks, "ebias": ebias,
        "fgb": np.ascontiguousarray(np.broadcast_to(f32(final_g).reshape(1, D), (128, D))),
    }
    in_maps = []
    for core in range(NCORES):
        b, half = core // 2, core % 2
        start = half * TOK
        xh = np.zeros((NTOK, D), np.float32)
        if half == 0:
            xh[HALO:] = x[b, 0:TOK]
        else:
            xh[:] = x[b, start - HALO:start + TOK]
        corr = np.ones((4, 16), np.float32)
        if half == 0:
            t = np.arange(16)
            for g in range(4):
                w = 2 ** (g + 1)
                corr[g] = w / np.minimum(t + 1, w)
        mp = dict(shared)
        mp["xh"] = xh
        mp["cT"] = fm(c[b])
        mp["valid"] = np.full((128, 1), 0.0 if half == 0 else 1.0, np.float32)
        mp["corr"] = np.ascontiguousarray(np.broadcast_to(corr.reshape(1, 64), (128, 64)))
        in_maps.append(mp)
    return in_maps


_NC_CACHE = {}


def kernel(**inputs):
    in_maps = prep_inputs(**inputs)
    if "nc" not in _NC_CACHE:
        _NC_CACHE["nc"] = build_nc()
    nc = _NC_CACHE["nc"]
    res = run_bass_kernel_spmd(nc, in_maps, core_ids=list(range(NCORES)))
    out = np.empty((NB, SEQ, D), np.float32)
    for core in range(NCORES):
        b, half = core // 2, core % 2
        out[b, half * TOK:(half + 1) * TOK] = res.results[core]["y"]
    return out
```

```python
import contextlib
import math
import numpy as np
import concourse.bass as bass
import concourse.mybir as mybir
from concourse.bass_utils import run_bass_kernel_spmd

F32 = mybir.dt.float32
BF16 = mybir.dt.bfloat16
AF = mybir.ActivationFunctionType
ALU = mybir.AluOpType

ENGS = ("pe", "act", "dve", "pool", "sp")

D = 1024
SEQ = 8192
NB = 4
NCORES = 8
TOK = 4096
HALO = 256
NTOK = TOK + HALO
TT = 512
EPS = 1e-6
NEG = -30000.0


class Op:
    __slots__ = ("eng", "fn", "reads", "writes", "chan", "deps", "signal", "ticket", "ndma", "idx")

    def __init__(self, eng, fn, reads, writes, chan, ndma):
        self.eng = eng
        self.fn = fn
        self.reads = reads
        self.writes = writes
        self.chan = chan
        self.ndma = ndma
        self.deps = []
        self.signal = False
        self.ticket = None


class Sched:
    def __init__(self):
        self.ops = []
        self.last_writer = {}
        self.readers = {}
        self.chan_last = {}
        self.marks = []

    def mark(self, name):
        self.marks.append((name, len(self.ops)))

    def add(self, eng, fn, reads=(), writes=(), chan=None, ndma=1):
        op = Op(eng, fn, tuple(reads), tuple(writes), chan, ndma)
        op.idx = len(self.ops)
        deps = []
        for k in op.reads:
            w = self.last_writer.get(k)
            if w is not None:
                deps.append(w)
        for k in op.writes:
            w = self.last_writer.get(k)
            if w is not None:
                deps.append(w)
            deps.extend(self.readers.get(k, ()))
        if chan is not None:
            p = self.chan_last.get(chan)
            if p is not None:
                deps.append(p)
            self.chan_last[chan] = op
        seen = set()
        best = {}
        for d in deps:
            if d is op or id(d) in seen:
                continue
            seen.add(id(d))
            if d.chan is not None:
                op.deps.append(d)
                d.signal = True
                continue
            if d.eng == "pe" and eng == "pe" and chan is None:
                continue
            cur = best.get(d.eng)
            if cur is None or d.idx > cur.idx:
                best[d.eng] = d
        for d in best.values():
            op.deps.append(d)
            d.signal = True
        for k in op.reads:
            self.readers.setdefault(k, []).append(op)
        for k in op.writes:
            self.last_writer[k] = op
            self.readers[k] = []
        self.ops.append(op)
        return op

    def emit(self, nc):
        cnt = {}
        chans = []
        for op in self.ops:
            if op.chan is not None:
                key = ("chan", op.chan)
                if key not in cnt:
                    chans.append(op.chan)
                cnt[key] = cnt.get(key, 0) + 16 * op.ndma
                op.ticket = cnt[key]
            elif op.signal:
                key = ("eng", op.eng)
                cnt[key] = cnt.get(key, 0) + 1
                op.ticket = cnt[key]
        per_eng = {e: [o for o in self.ops if o.eng == e] for e in ENGS}
        sems = {}
        with contextlib.ExitStack() as st:
            for e in ENGS:
                sems[("eng", e)] = st.enter_context(nc.semaphore("s_" + e))
            for c in chans:
                sems[("chan", c)] = st.enter_context(nc.semaphore("c_" + str(c)))
            block = st.enter_context(nc.Block())

            def body(ename):
                def run(eng):
                    waited = {}
                    for op in per_eng[ename]:
                        for d in op.deps:
                            skey = ("chan", d.chan) if d.chan is not None else ("eng", d.eng)
                            if waited.get(skey, 0) >= d.ticket:
                                continue
                            waited[skey] = d.ticket
                            eng.wait_ge(sems[skey], d.ticket)
                        ins = op.fn(eng)
                        if op.chan is not None:
                            if not isinstance(ins, (list, tuple)):
                                ins = [ins]
                            assert len(ins) == op.ndma, (len(ins), op.ndma)
                            for i_ in ins:
                                i_.then_inc(sems[("chan", op.chan)], 16)
                        elif op.signal:
                            ins.then_inc(sems[("eng", ename)], 1)
                    if ename == "sp":
                        for c in chans:
                            eng.wait_ge(sems[("chan", c)], cnt[("chan", c)])
                return run

            block.tensor(body("pe"))
            block.scalar(body("act"))
            block.vector(body("dve"))
            block.gpsimd(body("pool"))
            block.sync(body("sp"))
        return cnt


def build_nc(depth=4, ntiles=9, raw_out=False, maxops=None, verbose=False):
    nc = bass.Bass("TRN2", target_bir_lowering=False)
    S = Sched()

    def din(name, shape, dt=F32):
        return nc.dram_tensor(name, list(shape), dt, kind="ExternalInput").ap()

    xh = din("xh", [NTOK, D])
    cT = din("cT", [128, 8])
    vecs = din("vecs", [128, 80])
    adab = din("adab", [128, 112])
    ada_w = din("ada_w", [4, D, 3 * D])
    kv_ada_w = din("kv_ada_w", [D, 2 * D])
    a_w_in = din("a_w_in", [2, D, 4096])
    a_w_group = din("a_w_group", [2, 4, 512, 512])
    a_w_out = din("a_w_out", [2, 2048, D])
    w_kv = din("w_kv", [D, 256])
    b_w_in = din("b_w_in", [2, D, 2048])
    b_w_out = din("b_w_out", [2, D, D])
    sinks = din("sinks", [128, 32])
    ebias = din("ebias", [128, 16 * 256])
    valid = din("valid", [128, 1])
    corr = din("corr", [128, 64])
    fgb = din("fgb", [128, D])
    y = nc.dram_tensor("y", [TOK, D], F32, kind="ExternalOutput").ap()
    NSCR = 32
    wscr = nc.dram_tensor("wscr", [NSCR, 128, 8192], BF16, kind="Internal").ap()

    with contextlib.ExitStack() as st:
        def sb(name, shape, dt):
            return st.enter_context(nc.sbuf_tensor(name, list(shape), dt))

        def psb(name):
            return st.enter_context(nc.psum_tensor(name, [128, 512], F32))

        xs = sb("xs", [128, 8, TT], F32)
        NSTG = 4
        stg_in = sb("stg_in", [128, NSTG, D], F32)
        xsq = sb("xsq", [128, 8, TT], BF16)
        h = sb("h", [128, 8, TT], BF16)
        rs = sb("rs", [128, TT], F32)
        tmp = sb("tmp", [128, 2, TT], F32)
        u = sb("u", [128, 2, 4, 16 + TT], F32)
        sA = sb("sA", [128, 16 + TT], F32)
        sB = sb("sB", [128, 16 + TT], F32)
        yb = sb("yb", [128, 2, TT], BF16)
        bufA = sb("bufA", [128, 8, TT], BF16)
        bufB = sb("bufB", [128, 8, TT], BF16)
        m = sb("m", [128, 16, TT], BF16)
        uh = sb("uh", [128, 2, 16, 16], F32)
        kd = sb("kd", [128, 2, 9, 128], BF16)
        vx = sb("vx", [128, 9, 2, 66], BF16)
        E = sb("E", [128, 16, 256], BF16)
        pe_ = sb("pe", [128, 4, 512], BF16)
        pT = sb("pT", [128, 4, 512], BF16)
        otm = sb("otm", [128, 2, 256], BF16)
        den4 = sb("den4", [128, 2, 4], F32)
        identb = sb("identb", [128, 128], BF16)
        W = sb("W", [128, 3, 8192], BF16)
        WG = sb("WG", [128, 8192], BF16)
        ident = sb("ident", [128, 128], F32)
        ones = sb("ones", [128, 128], BF16)
        one1 = sb("one1", [1, 1], F32)
        cT_sb = sb("cT_sb", [128, 8], F32)
        cact = sb("cact", [128, 8], BF16)
        vecs_sb = sb("vecs_sb", [128, 80], F32)
        adab_sb = sb("adab_sb", [128, 112], F32)
        mods = sb("mods", [128, 112], F32)
        gs = sb("gs", [128, 40], F32)
        esink = sb("esink", [128, 32], F32)
        valid_sb = sb("valid_sb", [128, 1], F32)
        corr_sb = sb("corr_sb", [128, 64], F32)
        epsb = sb("epsb", [128, 1], F32)
        rst = sb("rst", [128, 4], F32)
        fgb_sb = sb("fgb_sb", [128, D], F32)
        pb = [psb("pb%d" % i) for i in range(8)]
        row_sb = tmp[0:1, :, :].rearrange("p a b -> p (a b)")
        ROWK = [("tmp", 0), ("tmp", 1)]
        print("sbuf bytes remaining:", nc.sbuf_bytes_remaining)

        V_NG, V_KVG, V_FG, V_AS = 0, 32, 40, 48

        tiles = [(0, HALO)] + [(HALO + i * TT, TT) for i in range(TOK // TT)]
        tiles = tiles[:ntiles]
        gblocks = [(ti_, blk_) for ti_, (t0_, T_) in enumerate(tiles) for blk_ in range(T_ // 128)]
        ldstate = {"n": 0}

        def issue_load():
            n = ldstate["n"]
            if n >= len(gblocks):
                return
            ldstate["n"] += 1
            ti_, blk_ = gblocks[n]
            s_ = n % NSTG
            r0 = tiles[ti_][0] + blk_ * 128
            S.add("sp", lambda e, s_=s_, r0=r0: e.dma_start(out=stg_in[:, s_, :], in_=xh[r0:r0 + 128, :]),
                  writes=[("stg_in", s_)], chan=("ldx", s_))

        for _ in range(NSTG):
            issue_load()
        S.add("sp", lambda e: [e.dma_start(out=cT_sb[:], in_=cT),
                               e.dma_start(out=vecs_sb[:], in_=vecs),
                               e.dma_start(out=adab_sb[:], in_=adab),
                               e.dma_start(out=esink[:], in_=sinks),
                               e.dma_start(out=valid_sb[:], in_=valid),
                               e.dma_start(out=corr_sb[:], in_=corr),
                               e.dma_start(out=fgb_sb[:], in_=fgb)],
              writes=["cT_sb", "vecs", "adab", "esink", "valid", "corr", "fgb"], chan="init", ndma=7)
        S.add("pool", lambda e: e.memset(ident[:], 0.0), writes=["ident"])
        S.add("pool", lambda e: e.affine_select(out=ident[:], in_=ident[:], pattern=[[-1, 128]],
                                                 compare_op=ALU.not_equal, fill=1.0, base=0,
                                                 channel_multiplier=1),
              reads=["ident"], writes=["ident"])
        S.add("pool", lambda e: e.memset(ones[:], 1.0), writes=["ones"])
        S.add("dve", lambda e: e.tensor_copy(out=identb[:], in_=ident[:]), reads=["ident"], writes=["identb"])
        S.add("pool", lambda e: e.memset(one1[:], 1.0), writes=["one1"])
        S.add("pool", lambda e: e.memset(epsb[:], EPS), writes=["epsb"])
        S.add("pool", lambda e: e.memset(uh[:].rearrange("p a b c -> p (a b c)"), 0.0), writes=[("uh", l, c) for l in range(2) for c in range(16)])
        S.add("pool", lambda e: e.memset(vx[:].rearrange("p a b c -> p (a b c)"), 1.0), writes=[("vx", r) for r in range(9)])
        S.add("act", lambda e: e.activation(out=cact[:], in_=cT_sb[:], func=AF.Silu),
              reads=["cT_sb"], writes=["cact"])
        for hf in range(2):
            stg = u[:, hf, :, :].rearrange("p a b -> p (a b)")[:, 0:2048]
            keys = [("u", hf, j) for j in range(4)]
            S.add("sp", lambda e, stg=stg, hf=hf: e.dma_start(out=stg, in_=ebias[:, hf * 2048:(hf + 1) * 2048]),
                  writes=keys, chan="ebias%d" % hf)
            S.add("act", lambda e, stg=stg, hf=hf: e.activation(
                out=E[:, hf * 8:(hf + 1) * 8, :].rearrange("p a b -> p (a b)"), in_=stg, func=AF.Exp),
                reads=keys, writes=["E"])
        S.add("act", lambda e: e.activation(out=esink[:], in_=esink[:], func=AF.Exp),
              reads=["esink"], writes=["esink"])

        wstate = {"scr": {}, "nscr": 0}

        def piece_srcs(name):
            kind = name[0]
            if kind == "ada":
                _, l_, cb = name
                return [(lambda s: v3(s, 8, 1024), kpn(ada_w[l_][:, cb * 1024:(cb + 1) * 1024]))], False
            if kind == "kvada":
                cb = name[1]
                return [(lambda s: v3(s, 8, 1024), kpn(kv_ada_w[:, cb * 1024:(cb + 1) * 1024]))], False
            if kind == "UZ":
                _, l_, g = name
                src = a_w_in[l_].rearrange("(k p) (z g n) -> p k z g n", p=128, z=2, g=4, n=512)
                return [(lambda s: v3(s, 8, 1024)[:, :, 0:512], src[:, :, 0, g, :]),
                        (lambda s: v3(s, 8, 1024)[:, :, 512:1024], src[:, :, 1, g, :])], True
            if kind == "WO":
                _, l_, half = name
                return [(lambda s: v3(s, 8, 1024), kpn(a_w_out[l_][half * 1024:(half + 1) * 1024, :]))], True
            if kind == "KV":
                f = lambda a, b: (lambda s: v3(s[:, 0:8 * 384], 8, 384)[:, :, a:b])
                return [(f(0, 64), kpn(w_kv[:, 0:64])), (f(64, 128), kpn(w_kv[:, 0:64])),
                        (f(128, 192), kpn(w_kv[:, 64:128])), (f(192, 256), kpn(w_kv[:, 64:128])),
                        (f(256, 384), kpn(w_kv[:, 128:256]))], True
            if kind == "BQZ":
                _, jl_, half = name
                return [(lambda s: v3(s, 8, 1024), kpn(b_w_in[jl_][:, half * 1024:(half + 1) * 1024]))], True
            if kind == "BO":
                return [(lambda s: v3(s, 8, 1024), kpn(b_w_out[name[1]]))], True
            if kind == "WG":
                l_ = name[1]
                return [(lambda s: s.rearrange("p (g i n) -> p g i n", g=4, i=4, n=512),
                         a_w_group[l_].rearrange("g (i p) n -> p g i n", p=128))], True
            raise KeyError(name)

        def wb_tile(name):
            kind = name[0]
            if kind in ("UZ", "WO", "WG"):
                return 1 + name[1]
            return 3

        def emit_load(name, slot_ap, key, ti_req=0):
            srcs, keep = piece_srcs(name)
            if keep and name in wstate["scr"]:
                si = wstate["scr"][name]
                S.add("pool", lambda e: e.dma_start(out=slot_ap, in_=wscr[si]),
                      reads=[("scr", si)], writes=[key], chan=("w", key))
            else:
                S.add("pool", lambda e: [e.dma_start(out=f(slot_ap), in_=s_) for f, s_ in srcs],
                      writes=[key], chan=("w", key), ndma=len(srcs))
                if keep and ti_req >= min(wb_tile(name), ntl - 1):
                    si = wstate["nscr"]
                    wstate["nscr"] += 1
                    assert si < NSCR
                    wstate["scr"][name] = si
                    S.add("sp", lambda e: e.dma_start(out=wscr[si], in_=slot_ap),
                          reads=[key], writes=[("scr", si)], chan=("wb", si % 4))

        ntl = min(ntiles, 1 + TOK // TT)
        reqs = []
        wg_uses = []
        req_ti = []
        wg_ti = []
        def mod_names(lm):
            if lm == 4:
                return [("kvada", 0), ("kvada", 1)]
            return [("ada", lm, cb) for cb in range(3)]

        hooks = {}
        if depth >= 2:
            hooks[(0, 0)] = 1
        if depth >= 3:
            hooks[(0, 1)] = 4
            hooks[(1, 0)] = 2
        if depth >= 4:
            hooks[(1, 1)] = 3
        hooks = {k: v for k, v in hooks.items() if k[0] < ntl and k[1] < min(depth, 2)}
        for ti_ in range(ntl):
            n0_, w0_ = len(reqs), len(wg_uses)
            for l_ in range(min(depth, 2)):
                if ti_ == 0 and l_ == 0:
                    reqs += mod_names(0)
                hk_ = mod_names(hooks[(ti_, l_)]) if (ti_, l_) in hooks else []
                hk_ = hk_ + [None] * (3 - len(hk_))
                reqs += [("UZ", l_, 3), ("UZ", l_, 2)]
                for g_, hn_ in zip((1, 0), hk_[:2]):
                    if hn_ is not None:
                        reqs.append(hn_)
                    reqs.append(("UZ", l_, g_))
                if hk_[2] is not None:
                    reqs.append(hk_[2])
                reqs += [("WO", l_, 0), ("WO", l_, 1)]
                wg_uses.append(("WG", l_))
            if depth >= 3:
                reqs.append(("KV",))
                if ti_ >= 1:
                    for jl_ in range(depth - 2):
                        reqs += [("BQZ", jl_, 0), ("BQZ", jl_, 1), ("BO", jl_)]
            req_ti += [ti_] * (len(reqs) - n0_)
            wg_ti += [ti_] * (len(wg_uses) - w0_)
        wq = {"issued": 0, "got": 0, "wg": 0}

        def wq_issue():
            i = wq["issued"]
            if i >= len(reqs):
                return
            wq["issued"] += 1
            sidx = i % 3
            emit_load(reqs[i], W[:, sidx, :], ("W", sidx), req_ti[i])

        def wq_get(name):
            i = wq["got"]
            assert reqs[i] == name, (reqs[i], name)
            assert i < wq["issued"], (i, name)
            wq["got"] += 1
            sidx = i % 3
            return W[:, sidx, :], ("W", sidx)

        def wq_done():
            wq_issue()

        def wg_issue():
            i = wq["wg"]
            if i >= len(wg_uses):
                return
            wq["wg"] += 1
            emit_load(wg_uses[i], WG[:], "WG", wg_ti[i])

        def v3(ap2d, k, n):
            return ap2d.rearrange("p (k n) -> p k n", k=k, n=n)

        def kpn(src2d):
            return src2d.rearrange("(k p) n -> p k n", p=128)

        bank_rr = {"i": 0}

        def mm_bank(lo=1, hi=7):
            n = hi - lo + 1
            b = lo + (bank_rr["i"] % n)
            bank_rr["i"] += 1
            return b

        def mod_piece(name, col0):
            sl, key = wq_get(name)
            w3 = v3(sl, 8, 1024)
            for n in range(2):
                b = mm_bank()
                for k in range(8):
                    S.add("pe", lambda e, b=b, k=k, n=n: e.matmul(
                        pb[b][0:1, :], lhsT=cact[:, k:k + 1], rhs=w3[:, k, n * 512:(n + 1) * 512],
                        start=(k == 0), stop=(k == 7)),
                        reads=[key, "cact"], writes=[("pb", b)])
                S.add("dve", lambda e, b=b, n=n: e.tensor_copy(out=row_sb[0:1, n * 512:(n + 1) * 512], in_=pb[b][0:1, :]),
                      reads=[("pb", b)], writes=ROWK)
            wq_done()
            b = mm_bank()
            for j in range(8):
                S.add("pe", lambda e, b=b, j=j: e.matmul(
                    pb[b][:, j:j + 1], lhsT=row_sb[0:1, j * 128:(j + 1) * 128], rhs=one1[0:1, 0:1],
                    start=True, stop=True),
                    reads=ROWK + ["one1"], writes=[("pb", b)])
            S.add("dve", lambda e, b=b: e.tensor_tensor(out=mods[:, col0:col0 + 8], in0=pb[b][:, 0:8],
                                                        in1=adab_sb[:, col0:col0 + 8], op=ALU.add),
                  reads=[("pb", b), "adab"], writes=["mods"])

        def mod_one(name):
            if name[0] == "ada":
                mod_piece(name, name[1] * 24 + name[2] * 8)
            else:
                mod_piece(name, 96 + name[1] * 8)

        def mods_finish(l):
            sc = (l * 24 + 8) if l < 4 else 104
            ng = (V_NG + l * 8) if l < 4 else V_KVG
            S.add("dve", lambda e: e.scalar_tensor_tensor(
                out=gs[:, l * 8:(l + 1) * 8], in0=mods[:, sc:sc + 8], scalar=1.0, in1=vecs_sb[:, ng:ng + 8],
                op0=ALU.add, op1=ALU.mult),
                reads=["mods", "vecs"], writes=["gs"])

        def mods_layer(l):
            for nm in mod_names(l):
                mod_one(nm)
            mods_finish(l)

        S.mark("mods")
        for _ in range(3):
            wq_issue()
        wg_issue()

        def shift_col(l):
            return (l * 24) if l < 4 else 96

        def gate_col(l):
            return l * 24 + 16

        def front_stats(T):
            b = 0
            for c in range(8):
                s_ = c
                S.add("act", lambda e, c=c, s_=s_: e.activation(out=xsq[:, s_, :T], in_=xs[:, c, :T], func=AF.Square),
                      reads=[("xs", c)], writes=[("xsq", s_)])
                S.add("pe", lambda e, c=c, s_=s_: e.matmul(pb[b][:, :T], lhsT=ones[:], rhs=xsq[:, s_, :T],
                                                            start=(c == 0), stop=(c == 7)),
                      reads=[("xsq", s_), "ones"], writes=[("pb", b)])
            S.add("act", lambda e: e.activation(out=rs[:, :T], in_=pb[b][:, :T], func=AF.Ln,
                                                bias=epsb[:], scale=1.0 / D),
                  reads=[("pb", b), "epsb"], writes=["rs"])
            S.add("act", lambda e: e.activation(out=rs[:, :T], in_=rs[:, :T], func=AF.Exp, scale=-0.5),
                  reads=["rs"], writes=["rs"])

        def keep_warm(n=8, extra=()):
            for _ in range(n):
                S.add("pe", lambda e: e.matmul(pb[0][:, :].rearrange("p (a b) -> p a b", a=2), lhsT=ones[:],
                                               rhs=E[:, 0:2, :], start=True, stop=True),
                      reads=["E", "ones"] + list(extra), writes=[("pb", 0)])

        def front_h(T, gcol, scol, warm=True):
            if warm:
                keep_warm(6)
                keep_warm(10, extra=["rs"])
            for c in range(8):
                s_ = c % 2
                S.add("dve", lambda e, c=c, s_=s_: e.scalar_tensor_tensor(
                    out=tmp[:, s_, :T], in0=xs[:, c, :T], scalar=gs[:, gcol + c:gcol + c + 1], in1=rs[:, :T],
                    op0=ALU.mult, op1=ALU.mult),
                    reads=[("xs", c), "rs", "gs"], writes=[("tmp", s_)])
                S.add("act", lambda e, c=c, s_=s_: e.activation(
                    out=h[:, c, :T], in_=tmp[:, s_, :T], func=AF.Identity, bias=mods[:, scol + c:scol + c + 1]),
                    reads=[("tmp", s_), "mods"], writes=[("h", c)])

        hkeys = [("h", c) for c in range(8)]

        def layer_a(l, ti, T):
            front_stats(T)
            front_h(T, l * 8, shift_col(l))
            wg_key = "WG"
            wg4 = WG[:].rearrange("p (g i n) -> p g i n", g=4, i=4, n=512)

            def uz(g):
                sl, key = wq_get(("UZ", l, g))
                w3 = v3(sl, 8, 1024)
                us = g % 2
                for j in range(4):
                    b = mm_bank()
                    for k in range(8):
                        S.add("pe", lambda e, b=b, k=k, j=j: e.matmul(
                            pb[b][:, :T], lhsT=w3[:, k, j * 128:(j + 1) * 128], rhs=h[:, k, :T],
                            start=(k == 0), stop=(k == 7)),
                            reads=[key, ("h", k)], writes=[("pb", b)])
                    ukey = ("u", us, j)
                    if ti == 0:
                        S.add("act", lambda e, b=b, j=j: e.activation(
                            out=u[:, us, j, 16:16 + T], in_=pb[b][:, :T], func=AF.Identity, scale=valid_sb[:, 0:1]),
                            reads=[("pb", b), "valid"], writes=[ukey])
                    else:
                        S.add("act", lambda e, b=b, j=j: e.activation(
                            out=u[:, us, j, 16:16 + T], in_=pb[b][:, :T], func=AF.Copy),
                            reads=[("pb", b)], writes=[ukey])
                    S.add("act", lambda e, j=j: e.activation(out=u[:, us, j, 0:16], in_=uh[:, l, g * 4 + j, :], func=AF.Copy),
                          reads=[("uh", l, g * 4 + j)], writes=[ukey])
                    if ti == 0:
                        S.add("act", lambda e, b=b, j=j: e.activation(
                            out=uh[:, l, g * 4 + j, :], in_=pb[b][:, T - 16:T], func=AF.Identity, scale=valid_sb[:, 0:1]),
                            reads=[("pb", b), "valid"], writes=[("uh", l, g * 4 + j)])
                    else:
                        S.add("act", lambda e, b=b, j=j: e.activation(
                            out=uh[:, l, g * 4 + j, :], in_=pb[b][:, T - 16:T], func=AF.Copy),
                            reads=[("pb", b)], writes=[("uh", l, g * 4 + j)])
                for j in range(4):
                    b = mm_bank()
                    for k in range(8):
                        S.add("pe", lambda e, b=b, k=k, j=j: e.matmul(
                            pb[b][:, :T], lhsT=w3[:, k, 512 + j * 128:512 + (j + 1) * 128], rhs=h[:, k, :T],
                            start=(k == 0), stop=(k == 7)),
                            reads=[key, ("h", k)], writes=[("pb", b)])
                    S.add("act", lambda e, b=b, j=j: e.activation(
                        out=bufB[:, us * 4 + j, :T], in_=pb[b][:, :T], func=AF.Silu),
                        reads=[("pb", b)], writes=[("bufB", us * 4 + j)])
                wq_done()

            def pool_g(g):
                us = g % 2
                w = 2 ** (g + 1)
                for j in range(4):
                    ukey = ("u", us, j)
                    src, skey = u[:, us, j, :], ukey
                    eng_ = "dve"
                    bufs = [(sA, "sA"), (sB, "sB")]
                    lo = 0
                    for k in range(g + 1):
                        d = 2 ** k
                        dst, dkey = bufs[k % 2]
                        S.add(eng_, lambda e, src=src, dst=dst, lo=lo, d=d: e.tensor_tensor(
                            out=dst[:, lo + d:16 + T], in0=src[:, lo + d:16 + T], in1=src[:, lo:16 + T - d], op=ALU.add),
                            reads=[skey], writes=[dkey])
                        src, skey = dst, dkey
                        lo += d
                    if ti == 1:
                        S.add(eng_, lambda e, src=src: e.tensor_tensor(
                            out=src[:, 16:32], in0=src[:, 16:32], in1=corr_sb[:, g * 16:(g + 1) * 16], op=ALU.mult),
                            reads=[skey, "corr"], writes=[skey])
                    S.add("dve", lambda e, src=src, j=j: e.scalar_tensor_tensor(
                        out=bufA[:, us * 4 + j, :T], in0=src[:, 16:16 + T], scalar=1.0 / w, in1=u[:, us, j, 16:16 + T],
                        op0=ALU.mult, op1=ALU.subtract),
                        reads=[skey, ukey], writes=[("bufA", us * 4 + j)])

            def ymat(g):
                us = g % 2
                for j in range(4):
                    b = mm_bank()
                    for i in range(4):
                        S.add("pe", lambda e, b=b, i=i, j=j: e.matmul(
                            pb[b][:, :T], lhsT=wg4[:, g, i, j * 128:(j + 1) * 128], rhs=bufA[:, us * 4 + i, :T],
                            start=(i == 0), stop=(i == 3)),
                            reads=[wg_key, ("bufA", us * 4 + i)], writes=[("pb", b)])
                    kk = g * 4 + j
                    ys = kk % 2
                    S.add("act", lambda e, b=b, kk=kk, ys=ys: e.activation(
                        out=yb[:, ys, :T], in_=pb[b][:, :T], func=AF.Identity,
                        scale=vecs_sb[:, V_AS + l * 16 + kk:V_AS + l * 16 + kk + 1]),
                        reads=[("pb", b), "vecs"], writes=[("yb", ys)])
                    S.add("dve", lambda e, j=j, kk=kk, ys=ys: e.tensor_tensor(
                        out=m[:, kk, :T], in0=yb[:, ys, :T], in1=bufB[:, us * 4 + j, :T], op=ALU.mult),
                        reads=[("yb", ys), ("bufB", us * 4 + j)], writes=[("m", kk)])

            hk = list(mod_names(hooks[(ti, l)])) if (ti, l) in hooks else []
            hk = hk + [None] * (3 - len(hk))
            uz(3)
            pool_g(3)
            uz(2)
            pool_g(2)
            ymat(3)
            if hk[0] is not None:
                mod_one(hk[0])
            uz(1)
            pool_g(1)
            ymat(2)
            if hk[1] is not None:
                mod_one(hk[1])
            uz(0)
            pool_g(0)
            ymat(1)
            ymat(0)
            if hk[2] is not None:
                mod_one(hk[2])
            if (ti, l) in hooks:
                mods_finish(hooks[(ti, l)])
            wg_issue()
            wo = []
            for half in range(2):
                sl, key = wq_get(("WO", l, half))
                wo.append((v3(sl, 8, 1024), key))
            gc = gate_col(l)
            for c in range(8):
                b = mm_bank()
                for k in range(16):
                    w3, key = wo[k // 8]
                    S.add("pe", lambda e, b=b, k=k, c=c, w3=w3: e.matmul(
                        pb[b][:, :T], lhsT=w3[:, k % 8, c * 128:(c + 1) * 128], rhs=m[:, k, :T],
                        start=(k == 0), stop=(k == 15)),
                        reads=[key, ("m", k)], writes=[("pb", b)])
                S.add("dve", lambda e, b=b, c=c: e.scalar_tensor_tensor(
                    out=xs[:, c, :T], in0=pb[b][:, :T], scalar=mods[:, gc + c:gc + c + 1], in1=xs[:, c, :T],
                    op0=ALU.mult, op1=ALU.add),
                    reads=[("pb", b), "mods", ("xs", c)], writes=[("xs", c)])
            wq_done()
            wq_done()

        def ring_of(gb):
            return 8 if gb < 0 else gb % 8

        def kv_part(ti, T):
            front_h(T, 32, shift_col(4))
            sl, key = wq_get(("KV",))
            w3 = v3(sl[:, 0:8 * 384], 8, 384)
            if ti == 0:
                blocks = [(-1, 1)]
            else:
                blocks = [(4 * (ti - 1) + i, i) for i in range(T // 128)]
            lb0 = blocks[0][1]
            nb = len(blocks)
            r0 = ring_of(blocks[0][0])
            rkeys = [("kd", ring_of(gb)) for gb, _ in blocks]
            vkeys = [("vx", ring_of(gb)) for gb, _ in blocks]
            for g in range(2):
                b = mm_bank()
                for k in range(8):
                    S.add("pe", lambda e, b=b, k=k, g=g: e.matmul(
                        pb[b][:, :nb * 128], lhsT=w3[:, k, g * 128:(g + 1) * 128],
                        rhs=h[:, k, lb0 * 128:(lb0 + nb) * 128], start=(k == 0), stop=(k == 7)),
                        reads=[key, ("h", k)], writes=[("pb", b)])
                S.add("dve", lambda e, b=b, g=g: e.tensor_copy(
                    out=kd[:, g, r0:r0 + nb, :], in_=pb[b][:, :nb * 128].rearrange("p (a b) -> p a b", a=nb)),
                    reads=[("pb", b)], writes=rkeys)
            b = mm_bank()
            for i, (gb, lb) in enumerate(blocks):
                for k in range(8):
                    S.add("pe", lambda e, b=b, k=k, i=i, lb=lb: e.matmul(
                        pb[b][:, i * 128:(i + 1) * 128], lhsT=h[:, k, lb * 128:(lb + 1) * 128],
                        rhs=w3[:, k, 256:384], start=(k == 0), stop=(k == 7)),
                        reads=[key, ("h", k)], writes=[("pb", b)])
            wq_done()
            S.add("dve", lambda e, b=b: e.tensor_copy(
                out=vx[:, r0:r0 + nb, :, 0:64],
                in_=pb[b][:, :nb * 128].rearrange("p (a g d) -> p a g d", a=nb, g=2)),
                reads=[("pb", b)], writes=vkeys)
            if ti == 0:
                S.add("dve", lambda e: e.tensor_scalar(
                    out=vx[:, 8, :, :].rearrange("p a b -> p (a b)"),
                    in0=vx[:, 8, :, :].rearrange("p a b -> p (a b)"),
                    scalar1=valid_sb[:, 0:1], scalar2=None, op0=ALU.mult),
                    reads=[("vx", 8), "valid"], writes=[("vx", 8)])

        def layer_b(l, ti, T):
            jl = l - 2
            front_h(T, l * 8, shift_col(l))
            qz = []
            for half in range(2):
                sl, key = wq_get(("BQZ", jl, half))
                qz.append((v3(sl, 8, 1024), key))
            for half in range(2):
                w3, key = qz[half]
                for c in range(8):
                    b = mm_bank()
                    for k in range(8):
                        S.add("pe", lambda e, b=b, k=k, c=c, w3=w3: e.matmul(
                            pb[b][:, :T], lhsT=w3[:, k, c * 128:(c + 1) * 128], rhs=h[:, k, :T],
                            start=(k == 0), stop=(k == 7)),
                            reads=[key, ("h", k)], writes=[("pb", b)])
                    if half == 0:
                        S.add("act", lambda e, b=b, c=c: e.activation(out=bufA[:, c, :T], in_=pb[b][:, :T], func=AF.Copy),
                              reads=[("pb", b)], writes=[("bufA", c)])
                    else:
                        S.add("act", lambda e, b=b, c=c: e.activation(out=bufB[:, c, :T], in_=pb[b][:, :T], func=AF.Silu),
                              reads=[("pb", b)], writes=[("bufB", c)])
                wq_done()
            units = [(blk, q) for blk in range(T // 128) for q in range(4)]
            sbank = {"i": 0}

            def scores(ui):
                blk, q = units[ui]
                gblk = 4 * (ti - 1) + blk
                g = q // 2
                banks = []
                for par in range(2):
                    banks.append(1 + (sbank["i"] % 4))
                    sbank["i"] += 1
                for hh in range(2):
                    c = 2 * q + hh
                    for kb in range(2):
                        rg = ring_of(gblk - 1 + kb)
                        for par in range(2):
                            b = banks[par]
                            r0 = par * 64
                            S.add("pe", lambda e, b=b, hh=hh, kb=kb, rg=rg, c=c, r0=r0: e.matmul(
                                pb[b][:, (hh * 2 + kb) * 128:(hh * 2 + kb + 1) * 128],
                                lhsT=kd[r0:r0 + 64, g, rg, :], rhs=bufA[r0:r0 + 64, c, blk * 128:(blk + 1) * 128],
                                start=True, stop=True),
                                reads=[("kd", rg), ("bufA", c)], writes=[("pb", b)])
                return banks

            def stage_x(ui, banks):
                blk, q = units[ui]
                for par in range(2):
                    b = banks[par]
                    ps_ = (ui * 2 + par) % 4
                    S.add("act", lambda e, b=b, ps_=ps_: e.activation(out=pe_[:, ps_, :], in_=pb[b][:, :], func=AF.Exp,
                                                                      scale=0.125),
                          reads=[("pb", b)], writes=[("pe", ps_)])
                    h0 = 4 * q + par
                    S.add("dve" if par == 0 else "pool", lambda e, ps_=ps_, h0=h0: e.tensor_tensor(
                        out=pT[:, ps_, :].rearrange("p (a b) -> p a b", a=2),
                        in0=pe_[:, ps_, :].rearrange("p (a b) -> p a b", a=2),
                        in1=E[:, h0:h0 + 3:2, :], op=ALU.mult),
                        reads=[("pe", ps_), "E"], writes=[("pT", ps_)])

            def stage_v(ui):
                blk, q = units[ui]
                gblk = 4 * (ti - 1) + blk
                g = q // 2
                ob = 5 + (ui % 2)
                sl_ = ui % 2
                for par in range(2):
                    ps_ = (ui * 2 + par) % 4
                    p4 = pT[:, ps_, :].rearrange("p (a k b) -> p a k b", a=2, k=2)
                    for hh in range(2):
                        hl = 2 * hh + par
                        for kb in range(2):
                            rg = ring_of(gblk - 1 + kb)
                            S.add("pe", lambda e, kb=kb, rg=rg, hh=hh, hl=hl, p4=p4: e.matmul(
                                pb[ob][:, hl * 128:hl * 128 + 65], lhsT=p4[:, hh, kb, :], rhs=vx[:, rg, g, 0:65],
                                start=(kb == 0), stop=(kb == 1)),
                                reads=[("vx", rg), ("pT", ps_)], writes=[("pb", ob)])
                ec = jl * 16 + 4 * q
                o4 = pb[ob][:, :].rearrange("p (a b) -> p a b", a=4)
                S.add("dve", lambda e: e.tensor_tensor(
                    out=den4[:, sl_, :], in0=o4[:, :, 64], in1=esink[:, ec:ec + 4], op=ALU.add),
                    reads=[("pb", ob), "esink"], writes=[("den4", sl_)])
                S.add("dve", lambda e: e.reciprocal(out=den4[:, sl_, :], in_=den4[:, sl_, :]),
                      reads=[("den4", sl_)], writes=[("den4", sl_)])
                S.add("dve", lambda e: e.tensor_tensor(
                    out=otm[:, sl_, :].rearrange("p (a b) -> p a b", a=4), in0=o4[:, :, 0:64],
                    in1=den4[:, sl_, :].unsqueeze(2).to_broadcast([128, 4, 64]), op=ALU.mult),
                    reads=[("pb", ob), ("den4", sl_)], writes=[("otm", sl_)])

            def stage_t(ui):
                blk, q = units[ui]
                sl_ = ui % 2
                tb = 7 if ui % 2 == 0 else 0
                tbv = pb[tb][:, :].bitcast(BF16)
                for c2 in range(2):
                    S.add("pe", lambda e, c2=c2: e.transpose(
                        out=tbv[:, c2 * 128:(c2 + 1) * 128], in_=otm[:, sl_, c2 * 128:(c2 + 1) * 128],
                        identity=identb[:]),
                        reads=[("otm", sl_), "identb"], writes=[("pb", tb)])
                S.add("dve", lambda e: e.tensor_tensor(
                    out=m[:, 2 * q:2 * q + 2, blk * 128:(blk + 1) * 128],
                    in0=tbv[:, 0:256].rearrange("p (a b) -> p a b", a=2),
                    in1=bufB[:, 2 * q:2 * q + 2, blk * 128:(blk + 1) * 128], op=ALU.mult),
                    reads=[("pb", tb), ("bufB", 2 * q), ("bufB", 2 * q + 1)],
                    writes=[("m", 2 * q), ("m", 2 * q + 1)])

            sl, key = wq_get(("BO", jl))
            nu = len(units)
            sb_ = {0: scores(0)}
            if nu > 1:
                sb_[1] = scores(1)
            stage_x(0, sb_[0])
            for ui in range(nu):
                if ui + 2 < nu:
                    sb_[ui + 2] = scores(ui + 2)
                if ui + 1 < nu:
                    stage_x(ui + 1, sb_[ui + 1])
                stage_v(ui)
                if ui >= 1:
                    stage_t(ui - 1)
            stage_t(nu - 1)
            w3 = v3(sl, 8, 1024)
            gc = gate_col(l)
            for c in range(8):
                b = mm_bank()
                for k in range(8):
                    S.add("pe", lambda e, b=b, k=k, c=c: e.matmul(
                        pb[b][:, :T], lhsT=w3[:, k, c * 128:(c + 1) * 128], rhs=m[:, k, :T],
                        start=(k == 0), stop=(k == 7)),
                        reads=[key, ("m", k)], writes=[("pb", b)])
                S.add("dve", lambda e, b=b, c=c: e.scalar_tensor_tensor(
                    out=xs[:, c, :T], in0=pb[b][:, :T], scalar=mods[:, gc + c:gc + c + 1], in1=xs[:, c, :T],
                    op0=ALU.mult, op1=ALU.add),
                    reads=[("pb", b), "mods", ("xs", c)], writes=[("xs", c)])
            wq_done()

        stcount = {"i": 0}
        S.mark("tiles")
        gbi = 0
        for ti, (t0, T) in enumerate(tiles):
            S.mark("tile%d" % ti)
            for blk in range(T // 128):
                s_ = gbi % NSTG
                gbi += 1
                for hc in range(2):
                    b = mm_bank()
                    for cc in range(4):
                        c = hc * 4 + cc
                        S.add("pe", lambda e, b=b, cc=cc, c=c, s_=s_: e.transpose(
                            out=pb[b][:, cc * 128:(cc + 1) * 128], in_=stg_in[:, s_, c * 128:(c + 1) * 128],
                            identity=ident[:]),
                            reads=[("stg_in", s_), "ident"], writes=[("pb", b)])
                    S.add("act", lambda e, b=b, hc=hc, blk=blk: e.activation(
                        out=xs[:, hc * 4:(hc + 1) * 4, blk * 128:(blk + 1) * 128],
                        in_=pb[b][:, :].rearrange("p (a b) -> p a b", a=4), func=AF.Copy),
                        reads=[("pb", b)], writes=[("xs", hc * 4 + cc) for cc in range(4)])
                issue_load()
            S.mark("tile%d-layers" % ti)
            nl = depth
            if nl >= 1:
                if ti == 0:
                    mods_layer(0)
                layer_a(0, ti, T)
            if nl >= 2:
                layer_a(1, ti, T)
            if nl >= 3:
                front_stats(T)
                kv_part(ti, T)
                if ti >= 1:
                    layer_b(2, ti, T)
                    if nl >= 4:
                        front_stats(T)
                        layer_b(3, ti, T)
            if ti == 0:
                continue
            nblk = T // 128
            if not raw_out:
                for c in range(8):
                    S.add("act", lambda e, c=c: e.activation(out=xsq[:, c, :T], in_=xs[:, c, :T], func=AF.Square),
                          reads=[("xs", c)], writes=[("xsq", c)])
                for blk in range(nblk):
                    for c in range(8):
                        S.add("pe", lambda e, c=c, blk=blk: e.matmul(
                            pb[0][:, blk:blk + 1], lhsT=xsq[:, c, blk * 128:(blk + 1) * 128], rhs=ones[:, 0:1],
                            start=(c == 0), stop=(c == 7)),
                            reads=[("xsq", c), "ones"], writes=[("pb", 0)])
                S.add("act", lambda e: e.activation(out=rst[:, 0:nblk], in_=pb[0][:, 0:nblk], func=AF.Ln,
                                                    bias=epsb[:], scale=1.0 / D),
                      reads=[("pb", 0), "epsb"], writes=["rst"])
                S.add("act", lambda e: e.activation(out=rst[:, 0:nblk], in_=rst[:, 0:nblk], func=AF.Exp, scale=-0.5),
                      reads=["rst"], writes=["rst"])
            for blk in range(nblk):
                so_ap = m[:, 4 * blk:4 * blk + 4, :].rearrange("p a b -> p (a b)").bitcast(F32)
                so_keys = [("m", 4 * blk + i_) for i_ in range(4)]
                for hc in range(2):
                    b = mm_bank(1, 7)
                    for cc in range(4):
                        c = hc * 4 + cc
                        S.add("pe", lambda e, b=b, cc=cc, c=c, blk=blk: e.transpose(
                            out=pb[b][:, cc * 128:(cc + 1) * 128], in_=xs[:, c, blk * 128:(blk + 1) * 128],
                            identity=ident[:]),
                            reads=[("xs", c), "ident"], writes=[("pb", b)])
                    if raw_out:
                        S.add("dve", lambda e, b=b, hc=hc, so_ap=so_ap: e.tensor_copy(
                            out=so_ap[:, hc * 512:(hc + 1) * 512], in_=pb[b][:, :]),
                            reads=[("pb", b)], writes=so_keys)
                    else:
                        S.add("dve", lambda e, b=b, hc=hc, so_ap=so_ap, blk=blk: e.scalar_tensor_tensor(
                            out=so_ap[:, hc * 512:(hc + 1) * 512], in0=pb[b][:, :],
                            scalar=rst[:, blk:blk + 1], in1=fgb_sb[:, hc * 512:(hc + 1) * 512],
                            op0=ALU.mult, op1=ALU.mult),
                            reads=[("pb", b), "rst", "fgb"], writes=so_keys)
                r0 = t0 - HALO + blk * 128
                S.add("sp", lambda e, so_ap=so_ap, r0=r0: e.dma_start(out=y[r0:r0 + 128, :], in_=so_ap),
                      reads=so_keys, chan=("sty", blk % 2))
        if verbose:
            for name, n in S.marks:
                print("mark", name, n)
            print("total ops", len(S.ops))
        if maxops is not None:
            S.ops = S.ops[:maxops]
        S.emit(nc)
    return nc


def _t5_buckets():
    i = np.arange(128)[:, None]
    j = np.arange(256)[None, :]
    n = np.maximum(i + 128 - j, 0)
    max_exact = 16
    large = max_exact + (np.log(np.maximum(n, 1) / max_exact) / math.log(128 / max_exact)
                         * (32 - max_exact)).astype(np.int32)
    large = np.minimum(large, 31)
    return np.where(n < max_exact, n, large).astype(np.int32)


def fm(v):
    v = np.asarray(v, np.float32).reshape(-1, 128)
    return np.ascontiguousarray(v.T)


def prep_inputs(x, c, norm_g, ada_w, ada_b, a_w_in, a_w_group, a_scale, a_w_out,
                kv_norm_g, kv_ada_w, kv_ada_b, w_kv, b_w_in, b_sinks, b_w_out,
                rel_bias, final_g):
    f32 = lambda a: np.ascontiguousarray(np.asarray(a, np.float32))
    x = f32(x)
    bk = _t5_buckets()
    rb = f32(rel_bias)
    pos = rb[bk]
    i = np.arange(128)[:, None]
    jj = np.arange(256)[None, :]
    rel = i + 128 - jj
    band = (rel >= 0) & (rel < 128)
    tab = np.where(band[:, :, None], pos, np.float32(NEG)).astype(np.float32)
    tab = tab.reshape(128, 2, 128, 16).transpose(2, 3, 1, 0)
    ebias = np.ascontiguousarray(tab.reshape(128, 16 * 256))
    vecs = np.concatenate([fm(norm_g[l]) for l in range(4)] + [fm(kv_norm_g), fm(final_g)]
                          + [fm(a_scale[l]) for l in range(2)], axis=1)
    adab = np.concatenate([fm(ada_b[l]) for l in range(4)] + [fm(kv_ada_b)], axis=1)
    sinks = np.ascontiguousarray(np.broadcast_to(f32(b_sinks).reshape(1, 32), (128, 32)))
    shared = {
        "vecs": f32(vecs), "adab": f32(adab), "ada_w": f32(ada_w), "kv_ada_w": f32(kv_ada_w),
        "a_w_in": f32(a_w_in), "a_w_group": f32(a_w_group), "a_w_out": f32(a_w_out),
        "w_kv": f32(w_kv), "b_w_in": f32(b_w_in), "b_w_out": f32(b_w_out),
        "sinks": sinks, "ebias": ebias,
        "fgb": np.ascontiguousarray(np.broadcast_to(f32(final_g).reshape(1, D), (128, D))),
    }
    in_maps = []
    for core in range(NCORES):
        b, half = core // 2, core % 2
        start = half * TOK
        xh = np.zeros((NTOK, D), np.float32)
        if half == 0:
            xh[HALO:] = x[b, 0:TOK]
        else:
            xh[:] = x[b, start - HALO:start + TOK]
        corr = np.ones((4, 16), np.float32)
        if half == 0:
            t = np.arange(16)
            for g in range(4):
                w = 2 ** (g + 1)
                corr[g] = w / np.minimum(t + 1, w)
        mp = dict(shared)
        mp["xh"] = xh
        mp["cT"] = fm(c[b])
        mp["valid"] = np.full((128, 1), 0.0 if half == 0 else 1.0, np.float32)
        mp["corr"] = np.ascontiguousarray(np.broadcast_to(corr.reshape(1, 64), (128, 64)))
        in_maps.append(mp)
    return in_maps


_NC_CACHE = {}


def kernel(**inputs):
    in_maps = prep_inputs(**inputs)
    if "nc" not in _NC_CACHE:
        _NC_CACHE["nc"] = build_nc()
    nc = _NC_CACHE["nc"]
    res = run_bass_kernel_spmd(nc, in_maps, core_ids=list(range(NCORES)))
    out = np.empty((NB, SEQ, D), np.float32)
    for core in range(NCORES):
        b, half = core // 2, core % 2
        out[b, half * TOK:(half + 1) * TOK] = res.results[core]["y"]
    return out
```
